# Optimizing a Trainium2 kernel written in Bass

```python
import math
import jax, jax.numpy as jnp
from jax import lax
import numpy as np

D_MODEL = 4096
BATCH = 4
SEQ = 4096
DEPTH = 2

HEAD_DIM = 128
ATTN_V_DIM = 2 * HEAD_DIM
ATTN_WIDTH = D_MODEL // 2
ATTN_HEADS = ATTN_WIDTH // ATTN_V_DIM
QK_WIDTH = ATTN_HEADS * 2 * HEAD_DIM
Q_BLOCK = 128
ROPE_THETA = 10000.0
SGU_WIDTH = D_MODEL // 4
SGU_CHUNK = 128
SGU_GROUPS = 8
SGU_GROUP_DIM = SGU_WIDTH // SGU_GROUPS
CONV_WIDTH = D_MODEL // 4
CONV_KERNEL = 31
N_BRANCHES = 3
FFN_DIM = ((8 * D_MODEL // 3 + 255) // 256) * 256
FFN_CONV_KERNEL = 3
NORM_EPS = 1e-6
LN_EPS = 1e-5
IN_SPLITS = [QK_WIDTH, 2 * QK_WIDTH, 2 * QK_WIDTH + ATTN_WIDTH,
             2 * QK_WIDTH + ATTN_WIDTH + 2 * SGU_WIDTH,
             2 * QK_WIDTH + ATTN_WIDTH + 2 * SGU_WIDTH + 2 * CONV_WIDTH]
IN_COLS = 2 * QK_WIDTH + ATTN_WIDTH + 2 * SGU_WIDTH + 2 * CONV_WIDTH + N_BRANCHES * D_MODEL

kernel_name = "hybrid_diffattn_sgu_conformer_convffn_adaln"


def rms_norm(x, g, eps=NORM_EPS):
    xf = x.astype(jnp.float32)
    y = xf * lax.rsqrt(jnp.mean(xf * xf, axis=-1, keepdims=True) + eps)
    return (y * g.astype(jnp.float32)).astype(x.dtype)


def layer_norm(x, g, b, eps=LN_EPS):
    xf = x.astype(jnp.float32)
    mu = jnp.mean(xf, axis=-1, keepdims=True)
    var = jnp.mean(jnp.square(xf - mu), axis=-1, keepdims=True)
    y = (xf - mu) * lax.rsqrt(var + eps)
    return (y * g.astype(jnp.float32) + b.astype(jnp.float32)).astype(x.dtype)


def rope(x, pos):
    half = HEAD_DIM // 2
    inv = ROPE_THETA ** (-jnp.arange(half, dtype=jnp.float32) / half)
    ang = pos.astype(jnp.float32)[:, None] * inv[None, :]
    cos, sin = jnp.cos(ang), jnp.sin(ang)
    xf = x.astype(jnp.float32)
    x1, x2 = xf[..., :half], xf[..., half:]
    out = jnp.concatenate([x1 * cos - x2 * sin, x2 * cos + x1 * sin], axis=-1)
    return out.astype(x.dtype)


def causal_depthwise_conv(x, w, b):
    k = w.shape[0]
    y = lax.conv_general_dilated(
        x, w[:, None, :].astype(x.dtype), window_strides=(1,), padding=[(k - 1, 0)],
        dimension_numbers=("NWC", "WIO", "NWC"), feature_group_count=x.shape[-1])
    return y + b.astype(x.dtype)


def diff_attention(q, k, v, lam):
    b_, h_, _, s_, d_ = q.shape
    nb = s_ // Q_BLOCK
    qb = q.reshape(b_, h_, 2, nb, Q_BLOCK, d_).transpose(3, 0, 1, 2, 4, 5)
    kpos = jnp.arange(s_)
    scale = d_ ** -0.5
    vf = v.astype(jnp.float32)

    def block(args):
        q_blk, i = args
        s = jnp.einsum("bhmqd,bhmkd->bhmqk", q_blk, k,
                       preferred_element_type=jnp.float32) * scale
        qpos = i * Q_BLOCK + jnp.arange(Q_BLOCK)
        mask = kpos[None, :] <= qpos[:, None]
        p = jax.nn.softmax(jnp.where(mask, s, -jnp.inf), axis=-1)
        a = p[:, :, 0] - lam * p[:, :, 1]
        return jnp.einsum("bhqk,bhkv->bhqv", a, vf)

    o = lax.map(block, (qb, jnp.arange(nb)))
    return o.transpose(1, 0, 3, 2, 4).reshape(b_, s_, h_, v.shape[-1])


def setup_inputs(seed: int = 0) -> dict:
    key = jax.random.key(seed)
    ks = jax.random.split(key, 32)
    D = D_MODEL

    def nrm(k, shape, scale):
        return jax.random.normal(k, shape, jnp.float32) * scale

    return {
        "x": nrm(ks[0], (BATCH, SEQ, D), 1.0),
        "c": nrm(ks[1], (BATCH, D), 1.0),
        "ada_w": nrm(ks[2], (DEPTH, D, 6 * D), 0.5 * D ** -0.5),
        "ada_b": nrm(ks[3], (DEPTH, 6 * D), 0.01),
        "norm1_g": 1.0 + nrm(ks[4], (DEPTH, D), 0.02),
        "w_in": nrm(ks[5], (DEPTH, D, IN_COLS), D ** -0.5),
        "b_gates": nrm(ks[6], (DEPTH, N_BRANCHES * D), 0.01),
        "lam_qk": nrm(ks[7], (DEPTH, 4, HEAD_DIM), 0.1),
        "attn_norm_g": 1.0 + nrm(ks[8], (DEPTH, ATTN_V_DIM), 0.02),
        "sgu_ln_g": 1.0 + nrm(ks[9], (DEPTH, SGU_WIDTH), 0.02),
        "sgu_ln_b": nrm(ks[10], (DEPTH, SGU_WIDTH), 0.01),
        "sgu_w": nrm(ks[11], (DEPTH, SGU_GROUPS, SGU_CHUNK, SGU_CHUNK), SGU_CHUNK ** -0.5),
        "sgu_b": 1.0 + nrm(ks[12], (DEPTH, SGU_GROUPS, SGU_CHUNK), 0.02),
        "conv_dw_w": nrm(ks[13], (DEPTH, CONV_KERNEL, CONV_WIDTH), CONV_KERNEL ** -0.5),
        "conv_dw_b": nrm(ks[14], (DEPTH, CONV_WIDTH), 0.01),
        "conv_ln_g": 1.0 + nrm(ks[15], (DEPTH, CONV_WIDTH), 0.02),
        "conv_ln_b": nrm(ks[16], (DEPTH, CONV_WIDTH), 0.01),
        "w_proj_attn": nrm(ks[17], (DEPTH, ATTN_WIDTH, D), ATTN_WIDTH ** -0.5),
        "w_proj_sgu": nrm(ks[18], (DEPTH, SGU_WIDTH, D), SGU_WIDTH ** -0.5),
        "w_proj_conv": nrm(ks[19], (DEPTH, CONV_WIDTH, D), CONV_WIDTH ** -0.5),
        "w_out": nrm(ks[20], (DEPTH, D, D), D ** -0.5),
        "norm2_g": 1.0 + nrm(ks[21], (DEPTH, D), 0.02),
        "ffn_up": nrm(ks[22], (DEPTH, D, 2 * FFN_DIM), D ** -0.5),
        "ffn_dw_w": nrm(ks[23], (DEPTH, FFN_CONV_KERNEL, FFN_DIM), FFN_CONV_KERNEL ** -0.5),
        "ffn_dw_b": nrm(ks[24], (DEPTH, FFN_DIM), 0.01),
        "ffn_down": nrm(ks[25], (DEPTH, FFN_DIM, D), FFN_DIM ** -0.5),
        "final_g": 1.0 + nrm(ks[26], (D,), 0.02),
    }


def reference(x, c, ada_w, ada_b, norm1_g, w_in, b_gates, lam_qk, attn_norm_g,
              sgu_ln_g, sgu_ln_b, sgu_w, sgu_b, conv_dw_w, conv_dw_b, conv_ln_g, conv_ln_b,
              w_proj_attn, w_proj_sgu, w_proj_conv, w_out, norm2_g,
              ffn_up, ffn_dw_w, ffn_dw_b, ffn_down, final_g):
    B, S, D = x.shape
    pos = jnp.arange(S)
    c_act = jax.nn.silu(c)
    tri = jnp.tril(jnp.ones((SGU_CHUNK, SGU_CHUNK), dtype=bool))

    for l in range(DEPTH):
        mod = c_act @ ada_w[l] + ada_b[l]
        sh1, sc1, g1, sh2, sc2, g2 = [m[:, None, :] for m in jnp.split(mod, 6, axis=-1)]

        h = rms_norm(x, norm1_g[l]) * (1.0 + sc1) + sh1
        z = h @ w_in[l]
        zq, zk, zv, zs, zc, zg = jnp.split(z, IN_SPLITS, axis=-1)

        q = zq.reshape(B, S, ATTN_HEADS, 2, HEAD_DIM).transpose(0, 2, 3, 1, 4)
        k = zk.reshape(B, S, ATTN_HEADS, 2, HEAD_DIM).transpose(0, 2, 3, 1, 4)
        v = zv.reshape(B, S, ATTN_HEADS, ATTN_V_DIM).transpose(0, 2, 1, 3)
        q, k = rope(q, pos), rope(k, pos)
        lam_init = 0.8 - 0.6 * math.exp(-0.3 * l)
        lq = lam_qk[l].astype(jnp.float32)
        lam = jnp.exp(jnp.sum(lq[0] * lq[1])) - jnp.exp(jnp.sum(lq[2] * lq[3])) + lam_init
        o = diff_attention(q, k, v, lam).astype(x.dtype)
        o_attn = (rms_norm(o, attn_norm_g[l]) * (1.0 - lam_init)).reshape(B, S, ATTN_WIDTH)

        zs = jax.nn.gelu(zs, approximate=False)
        u, sv = jnp.split(zs, 2, axis=-1)
        sv = layer_norm(sv, sgu_ln_g[l], sgu_ln_b[l])
        sv = sv.reshape(B, S // SGU_CHUNK, SGU_CHUNK, SGU_GROUPS, SGU_GROUP_DIM)
        ws = jnp.where(tri[None], sgu_w[l], 0.0).astype(sv.dtype)
        sp = jnp.einsum("gts,bnsgc->bntgc", ws, sv) + sgu_b[l].T[None, None, :, :, None]
        o_sgu = u * sp.reshape(B, S, SGU_WIDTH)

        ca, cb = jnp.split(zc, 2, axis=-1)
        y = ca * jax.nn.sigmoid(cb)
        y = causal_depthwise_conv(y, conv_dw_w[l], conv_dw_b[l])
        o_conv = jax.nn.silu(layer_norm(y, conv_ln_g[l], conv_ln_b[l]))

        gates = jax.nn.sigmoid(zg + b_gates[l]).reshape(B, S, N_BRANCHES, D)
        merged = (gates[:, :, 0] * (o_attn @ w_proj_attn[l])
                  + gates[:, :, 1] * (o_sgu @ w_proj_sgu[l])
                  + gates[:, :, 2] * (o_conv @ w_proj_conv[l]))
        x = x + g1 * (merged @ w_out[l])

        h2 = rms_norm(x, norm2_g[l]) * (1.0 + sc2) + sh2
        fa, fb = jnp.split(h2 @ ffn_up[l], 2, axis=-1)
        fa = causal_depthwise_conv(fa, ffn_dw_w[l], ffn_dw_b[l])
        x = x + g2 * ((jax.nn.silu(fa) * fb) @ ffn_down[l])

    return rms_norm(x, final_g)
```

```python
import math
from contextlib import ExitStack
import numpy as np
import concourse.bass as bass
import concourse.mybir as mybir
from concourse.bass_utils import run_bass_kernel_spmd

F32 = mybir.dt.float32
BF16 = mybir.dt.bfloat16
AF = mybir.ActivationFunctionType
ALU = mybir.AluOpType
AX = mybir.AxisListType


class Buf:
    __slots__ = ("name", "w", "r", "excl")

    def __init__(self, name="", excl=False):
        self.name = name
        self.w = None
        self.r = {}
        self.excl = excl


class Sched:
    ENG = ("pe", "act", "dve", "pool", "sp")

    def __init__(self, nc, n_dma_sems=40):
        self.nc = nc
        self.ops = {e: [] for e in self.ENG}
        self.count = {e: 0 for e in self.ENG}
        self.known = {e: {} for e in self.ENG}
        self.sems = {}
        self.n_dma_sems = n_dma_sems
        self.dma_cnt = [0] * n_dma_sems
        self.dma_rr = 0
        self.pool_rr = 0
        self.n_pool_sems = 8
        self.cc_cnt = 0
        self.n_waits = 0
        self.n_ops = 0

    def alloc_sems(self, stack):
        self.sems["cc"] = stack.enter_context(self.nc.semaphore("s_cc"))
        for e in self.ENG:
            self.sems[e] = stack.enter_context(self.nc.semaphore("s_" + e))
        for i in range(self.n_dma_sems):
            self.sems[("d", i)] = stack.enter_context(self.nc.semaphore("s_d%d" % i))

    def _need(self, eng, deps):
        kn = self.known[eng]
        best = {}
        for d in deps:
            if d is None:
                continue
            k, v = d
            if eng == "pe" and k == "pe":
                continue
            if kn.get(k, 0) >= v:
                continue
            if best.get(k, 0) < v:
                best[k] = v
        for k, v in best.items():
            kn[k] = v
            h = self.sems[k]
            self.ops[eng].append(lambda e, h=h, v=v: e.wait_ge(h, v))
            self.n_waits += 1

    @staticmethod
    def _deps(reads, writes):
        deps = []
        for b in reads:
            deps.append(b.w)
        for b in writes:
            deps.append(b.w)
            for kv in b.r.items():
                deps.append(kv)
        return deps

    @staticmethod
    def _commit(key, val, reads, writes):
        for b in reads:
            if b.r.get(key, 0) < val:
                b.r[key] = val
        for b in writes:
            b.w = (key, val)
            b.r = {}

    def op(self, eng, fn, reads=(), writes=()):
        ex = [b for b in reads if b.excl]
        if ex:
            reads = [b for b in reads if not b.excl]
            writes = list(writes) + ex
        self._need(eng, self._deps(reads, writes))
        self.count[eng] += 1
        val = self.count[eng]
        h = self.sems[eng]
        self.ops[eng].append(lambda e, fn=fn, h=h: fn(e).then_inc(h, 1))
        self._commit(eng, val, reads, writes)
        self.n_ops += 1

    def dma(self, q, out_ap, in_ap, reads=(), writes=(), **kw):
        if q == "pool":
            i = self.pool_rr
            self.pool_rr = (self.pool_rr + 1) % self.n_pool_sems
        else:
            i = self.n_pool_sems + self.dma_rr
            self.dma_rr = (self.dma_rr + 1) % (self.n_dma_sems - self.n_pool_sems)
        key = ("d", i)
        deps = self._deps(reads, writes)
        if self.dma_cnt[i] > 0:
            deps.append((key, self.dma_cnt[i]))
        self._need(q, deps)
        self.dma_cnt[i] += 16
        val = self.dma_cnt[i]
        h = self.sems[key]
        self.ops[q].append(
            lambda e, o=out_ap, a=in_ap, h=h, kw=kw: e.dma_start(out=o, in_=a, **kw).then_inc(h, 16))
        self._commit(key, val, reads, writes)
        self.n_ops += 1

    def cc(self, in_ap, out_ap, groups, reads=(), writes=()):
        key = "cc"
        self._need("pool", self._deps(reads, writes))
        self.cc_cnt += 1
        val = self.cc_cnt
        h = self.sems[key]
        self.ops["pool"].append(lambda e, i=in_ap, o=out_ap, h=h, g=groups: e.collective_compute(
            "AllGather", ALU.bypass, replica_groups=g, ins=[i], outs=[o]).then_inc(h, 1))
        self._commit(key, val, reads, writes)
        self.n_ops += 1

    def barrier(self, engines=None):
        allv = [(e, self.count[e]) for e in self.ENG if self.count[e] > 0]
        if self.cc_cnt:
            allv.append(("cc", self.cc_cnt))
        allv += [(("d", i), self.dma_cnt[i]) for i in range(self.n_dma_sems) if self.dma_cnt[i] > 0]
        for e in (engines or self.ENG):
            self._need(e, allv)

    def emit(self, block):
        ops = self.ops

        @block.tensor
        def _(e):
            for f in ops["pe"]:
                f(e)

        @block.scalar
        def _(e):
            for f in ops["act"]:
                f(e)

        @block.vector
        def _(e):
            for f in ops["dve"]:
                f(e)

        @block.gpsimd
        def _(e):
            for f in ops["pool"]:
                f(e)

        @block.sync
        def _(e):
            for f in ops["sp"]:
                f(e)


class Ring:
    def __init__(self, items):
        self.items = items
        self.i = 0

    def next(self):
        it = self.items[self.i]
        self.i = (self.i + 1) % len(self.items)
        return it


class StopBuild(Exception):
    pass


class Cfg:
    def __init__(self, D=4096, S=4096, B=4, DEPTH=2, stop=None, pair=False):
        self.stop = stop
        self.pair = pair
        self.ncores = 8
        self.ccrows = 512
        self.D, self.S, self.B, self.DEPTH = D, S, B, DEPTH
        self.DC = D // 128
        self.AW = D // 2
        self.H = self.AW // 256
        self.QK = self.H * 256
        self.SW = D // 4
        self.SG = self.SW // 128
        self.CW = D // 4
        self.CC = self.CW // 128
        self.CK = 31
        self.F = ((8 * D // 3 + 255) // 256) * 256
        self.FC = self.F // 128
        self.IN = 2 * self.QK + self.AW + 2 * self.SW + 2 * self.CW + 3 * D
        self.oq = 0
        self.ok = self.QK
        self.ov = 2 * self.QK
        self.ou = self.ov + self.AW
        self.osv = self.ou + self.SW
        self.oca = self.osv + self.SW
        self.ocb = self.oca + self.CW
        self.og = self.ocb + self.CW
        self.NT = S // 2 if pair else S


W_NAMES = ["ada_w", "ada_b", "norm1_g", "w_in", "b_gates", "lam_qk", "attn_norm_g", "sgu_ln_g", "sgu_ln_b",
           "sgu_w", "sgu_b", "conv_dw_w", "conv_dw_b", "conv_ln_g", "conv_ln_b", "w_proj_attn", "w_proj_sgu",
           "w_proj_conv", "w_out", "norm2_g", "ffn_up", "ffn_dw_w", "ffn_dw_b", "ffn_down", "final_g"]


def w_shapes(c):
    L, D = c.DEPTH, c.D
    return {
        "ada_w": [L, D, 6 * D], "ada_b": [L, 6 * D], "norm1_g": [L, D], "w_in": [L, D, c.IN],
        "b_gates": [L, 3 * D], "lam_qk": [L, 4, 128], "attn_norm_g": [L, 256], "sgu_ln_g": [L, c.SW],
        "sgu_ln_b": [L, c.SW], "sgu_w": [L, c.SG, 128, 128], "sgu_b": [L, c.SG, 128],
        "conv_dw_w": [L, c.CK, c.CW], "conv_dw_b": [L, c.CW], "conv_ln_g": [L, c.CW], "conv_ln_b": [L, c.CW],
        "w_proj_attn": [L, c.AW, D], "w_proj_sgu": [L, c.SW, D], "w_proj_conv": [L, c.CW, D],
        "w_out": [L, D, D], "norm2_g": [L, D], "ffn_up": [L, D, 2 * c.F], "ffn_dw_w": [L, 3, c.F],
        "ffn_dw_b": [L, c.F], "ffn_down": [L, c.F, D], "final_g": [D],
    }


def make_consts(c, hidx=0):
    half = 64
    inv = (10000.0 ** (-np.arange(half, dtype=np.float32) / np.float32(half))).astype(np.float32)
    pos = np.arange(c.S, dtype=np.float32)
    ang = (pos[:, None] * inv[None, :]).astype(np.float32)
    cos = np.cos(ang.astype(np.float64)).astype(np.float32).T
    sin = np.sin(ang.astype(np.float64)).astype(np.float32).T
    cos_tab = np.concatenate([cos, cos], 0)
    sin_tab = np.concatenate([-sin, sin], 0)
    ident = np.eye(128, dtype=np.float32)
    pswap = np.zeros((128, 128), np.float32)
    for m in range(128):
        pswap[(m + 64) % 128, m] = 1.0
    tri = np.triu(np.ones((128, 128), np.float32))
    p0 = hidx * c.NT
    return {"c_ident": ident, "c_pswap": pswap, "c_tri": tri,
            "c_flag": np.full((128, 1), float(hidx), np.float32),
            "c_cos": np.ascontiguousarray(cos_tab[:, p0:p0 + c.NT]),
            "c_sin": np.ascontiguousarray(sin_tab[:, p0:p0 + c.NT])}


def build_program(c):
    nc = bass.Bass("TRN2", target_bir_lowering=False)
    D, NT, DC = c.D, c.NT, c.DC
    L = c.DEPTH

    def din(name, shape):
        return nc.dram_tensor(name, list(shape), F32, kind="ExternalInput").ap()

    x_in = din("x", [NT, D])
    c_in = din("c", [1, D])
    W = {n: din(n, s) for n, s in w_shapes(c).items()}
    k_ident = din("c_ident", [128, 128])
    k_pswap = din("c_pswap", [128, 128])
    k_tri = din("c_tri", [128, 128])
    k_cos = din("c_cos", [128, NT])
    k_sin = din("c_sin", [128, NT])
    k_flag = din("c_flag", [128, 1])
    PAIR = c.pair
    RG = [[2 * i, 2 * i + 1] for i in range(c.ncores // 2)]
    y_out = nc.dram_tensor("y", [NT, D], F32, kind="ExternalOutput").ap()

    def dscr(name, shape, dt):
        return nc.dram_tensor(name, list(shape), dt, kind="Internal").ap()

    XS = dscr("xs", [NT, D], F32)
    MODS = dscr("mods", [L, 6 * D], F32)
    QT = dscr("qt", [c.QK // 128, 128, NT], BF16)
    KT = dscr("kt", [c.QK // 128, 128, NT], BF16)
    VV = dscr("vv", [NT, c.AW], BF16)
    UT = dscr("ut", [c.SG, 128, NT], BF16)
    SVR = dscr("svr", [NT, c.SW], F32)
    YT = dscr("yt", [c.CC, 128, 32 + NT], F32)
    GT = dscr("gt", [3 * DC, 128, NT], BF16)
    OAT = dscr("oat", [c.AW // 128, 128, NT], BF16)
    OST = dscr("ost", [c.SG, 128, NT], BF16)
    OCT = dscr("oct", [c.CC, 128, NT], BF16)
    ACTT = dscr("actt", [c.FC, 128, NT], BF16)
    HALO = 32
    NRC = (NT // 128) if PAIR else 0
    if PAIR:
        PR = c.ccrows
        KTG = dscr("ktg", [c.QK // PR, 2 * PR, NT], BF16)
        VVG = dscr("vvg", [NT // PR, 2 * PR, c.AW], BF16)
        YTAIL = dscr("ytail", [c.CC, 128 * 32], F32)
        YTG = dscr("ytg", [2 * c.CC, 128 * 32], F32)
        HALOX = dscr("halox", [c.FC, 128 * 2], F32)
        HG = dscr("hg", [2 * c.FC, 128 * 2], F32)

    with ExitStack() as st:
        S = Sched(nc)
        S.alloc_sems(st)

        uniq = [0]

        def sb(stack, name, shape, dt):
            uniq[0] += 1
            return stack.enter_context(nc.sbuf_tensor("%s_%d" % (name, uniq[0]), list(shape), dt))

        ident = sb(st, "ident", [128, 128], F32)
        ones_f = sb(st, "ones_f", [128, 128], F32)
        ones_b = sb(st, "ones_b", [128, 128], BF16)
        pswap = sb(st, "pswap", [128, 128], BF16)
        tri = sb(st, "tri", [128, 128], BF16)
        zero30 = sb(st, "zero30", [128, HALO], F32)
        flag = sb(st, "flag", [128, 1], F32)
        modT = sb(st, "modT", [128, L, 6 * DC], F32)
        gsc = sb(st, "gsc", [128, 2, DC], F32)
        stat1 = sb(st, "stat1", [128, NT // 128, 2], F32)
        stat2 = sb(st, "stat2", [128, NT // 128, 2], F32)
        bC = Buf("consts")
        b_modT, b_gsc, b_stat = Buf("modT"), Buf("gsc"), Buf("stat")
        ps = [st.enter_context(nc.psum_tensor("ps%d" % i, [128, 512], F32)) for i in range(8)]
        bps = [Buf("ps%d" % i, excl=True) for i in range(8)]
        block = st.enter_context(nc.Block())

        S.dma("sp", ident[:], k_ident, writes=[bC])
        S.dma("sp", flag[:], k_flag, writes=[bC])
        S.dma("pool", pswap[:], k_pswap, writes=[bC])
        S.dma("pool", tri[:], k_tri, writes=[bC])
        S.op("dve", lambda e: e.memset(ones_f[:], 1.0), writes=[bC])
        S.op("dve", lambda e: e.memset(ones_b[:], 1.0), writes=[bC])
        S.op("dve", lambda e: e.memset(zero30[:], 0.0), writes=[bC])
        for cc in range(c.CC):
            S.dma("sp", YT[cc][:, 0:HALO], zero30[:], reads=[bC])
        S.barrier()

        def load_featmajor(ph, dst_fn, vec_ap, n_ch, bank, tag):
            for c0 in range(0, n_ch, 128):
                n = min(128, n_ch - c0)
                tmp = sb(ph, "lf_%s_%d" % (tag, c0), [128, 128], F32)
                bt = Buf()
                S.dma("sp", tmp[0:n, :], vec_ap[c0 * 128:(c0 + n) * 128].rearrange("(c p) -> c p", p=128),
                      writes=[bt])
                S.op("pe", lambda e, tmp=tmp, n=n: e.transpose(out=ps[bank][:, 0:n], in_=tmp[0:n, :],
                                                                identity=ident[0:n, 0:n]),
                     reads=[bt, bC], writes=[bps[bank]])
                d = dst_fn(c0, n)
                S.op("dve", lambda e, d=d, n=n: e.tensor_copy(out=d, in_=ps[bank][:, 0:n]),
                     reads=[bps[bank]], writes=[bC])

        cTb = sb(st, "cTb", [128, DC], BF16)
        b_cT = Buf("cTb")

        def make_ada(l, ph, bank):
            wA = [sb(ph, "wA%d" % i, [128, DC, 512], BF16) for i in range(2)]
            bwA = [Buf(), Buf()]
            mrow = Ring([(sb(ph, "mrow%d" % i, [1, 512], F32), Buf()) for i in range(2)])
            abr = Ring([(sb(ph, "abr%d" % i, [1, 512], F32), Buf()) for i in range(2)])
            ng = 6 * D // 512
            state = {"i": 0}

            def load(i):
                S.dma("pool", wA[i % 2][:], W["ada_w"][l][:, i * 512:(i + 1) * 512].rearrange(
                    "(kc p) n -> p kc n", p=128), writes=[bwA[i % 2]])

            def step():
                i = state["i"]
                if i >= ng:
                    return False
                if i == 0:
                    load(0)
                if i + 1 < ng:
                    load(i + 1)
                ab, bab = abr.next()
                S.dma("sp", ab[:], W["ada_b"][l:l + 1, i * 512:(i + 1) * 512], writes=[bab])

                def mm(e, i=i):
                    for kc in range(DC):
                        ins = e.matmul(ps[bank][0:1, :], lhsT=cTb[:, kc:kc + 1], rhs=wA[i % 2][:, kc, :],
                                       start=(kc == 0), stop=(kc == DC - 1))
                    return ins
                S.op("pe", mm, reads=[b_cT, bwA[i % 2]], writes=[bps[bank]])
                mr, bmr = mrow.next()
                S.op("dve", lambda e, mr=mr, ab=ab: e.tensor_tensor(out=mr[:], in0=ps[bank][0:1, :], in1=ab[:],
                                                                    op=ALU.add),
                     reads=[bps[bank], bab], writes=[bmr])
                S.dma("sp", MODS[l:l + 1, i * 512:(i + 1) * 512], mr[:], reads=[bmr])
                state["i"] = i + 1
                return True
            return step

        def load_modT(l):
            with ExitStack() as ph2:
                load_featmajor(ph2, lambda c0, n, l=l: modT[:, l, c0:c0 + n], MODS[l], 6 * DC, 7, "m%d" % l)
                S.barrier()

        with ExitStack() as ph:
            cT = sb(ph, "cT", [128, DC], F32)
            load_featmajor(ph, lambda c0, n: cT[:, c0:c0 + n], c_in[0], DC, 7, "c")
            S.op("act", lambda e: e.activation(out=cTb[:], in_=cT[:], func=AF.Silu), reads=[bC], writes=[b_cT])
            step0 = make_ada(0, ph, 0)
            while step0():
                pass
            S.barrier()
        load_modT(0)

        def mod_vec(l, i):
            return modT[:, l, i * DC:(i + 1) * DC]

        def norm_to_hT(ph_bufs, xsrc, r0, nblk_col, hT, b_hT, sub, sh_i, l, banks):
            xb, b_xb, junk, b_junk, ss, b_ss = ph_bufs
            S.dma("sp", xb[:], xsrc[r0:r0 + 128, :], writes=[b_xb])
            S.op("act", lambda e: e.activation(out=junk[:], in_=xb[:], func=AF.Square, accum_out=ss[:, 0:1]),
                 reads=[b_xb], writes=[b_junk, b_ss])
            S.op("act", lambda e: e.activation(out=ss[:, 1:2], in_=ss[:, 0:1], func=AF.Sqrt, scale=1.0 / D,
                                               bias=1e-6), reads=[b_ss], writes=[b_ss])
            S.op("dve", lambda e: e.reciprocal(out=ss[:, 2:3], in_=ss[:, 1:2]), reads=[b_ss], writes=[b_ss])
            S.op("dve", lambda e: e.tensor_scalar(out=xb[:], in0=xb[:], scalar1=ss[:, 2:3], scalar2=None,
                                                  op0=ALU.mult), reads=[b_ss, b_xb], writes=[b_xb])
            for g in range(DC // 4):
                bank = banks[g % len(banks)]

                def tr(e, g=g, bank=bank):
                    for j in range(4):
                        kc = g * 4 + j
                        ins = e.transpose(out=ps[bank][:, j * 128:(j + 1) * 128],
                                          in_=xb[:, kc * 128:(kc + 1) * 128], identity=ident[:])
                    return ins
                S.op("pe", tr, reads=[b_xb, bC], writes=[bps[bank]])
                for j in range(4):
                    kc = g * 4 + j
                    dst = hT[:, kc, nblk_col * 128:(nblk_col + 1) * 128]
                    src = ps[bank][:, j * 128:(j + 1) * 128]
                    if g % 2 == 0:
                        S.op("act", lambda e, dst=dst, src=src, kc=kc: e.activation(
                            out=dst, in_=src, func=AF.Identity, scale=gsc[:, sub, kc:kc + 1],
                            bias=modT[:, l, sh_i * DC + kc:sh_i * DC + kc + 1]),
                            reads=[bps[bank], b_gsc, b_modT], writes=[b_hT])
                    else:
                        S.op("dve", lambda e, dst=dst, src=src, kc=kc: e.tensor_scalar(
                            out=dst, in0=src, scalar1=gsc[:, sub, kc:kc + 1],
                            scalar2=modT[:, l, sh_i * DC + kc:sh_i * DC + kc + 1], op0=ALU.mult, op1=ALU.add),
                            reads=[bps[bank], b_gsc, b_modT], writes=[b_hT])

        def layer(l):
            lam_init = 0.8 - 0.6 * math.exp(-0.3 * l)
            xcur = x_in if l == 0 else XS

            if l > 0:
                load_modT(l)
            with ExitStack() as ph:
                ng1 = sb(ph, "ng1", [128, DC], F32)
                load_featmajor(ph, lambda c0, n: ng1[:, c0:c0 + n], W["norm1_g"][l], DC, 7, "n1")
                ng2 = sb(ph, "ng2", [128, DC], F32)
                load_featmajor(ph, lambda c0, n: ng2[:, c0:c0 + n], W["norm2_g"][l], DC, 7, "n2")
                for sub, ng, sci in ((0, ng1, 1), (1, ng2, 4)):
                    S.op("dve", lambda e, sub=sub, ng=ng, sci=sci: e.scalar_tensor_tensor(
                        out=gsc[:, sub, :], in0=mod_vec(l, sci), scalar=1.0, in1=ng[:], op0=ALU.add,
                        op1=ALU.mult), reads=[bC, b_modT], writes=[b_gsc])
                S.barrier()

            with ExitStack() as ph:
                T1 = min(1024, NT)
                NS = T1 // 512
                hT1 = sb(ph, "hT1", [128, DC, T1], BF16)
                b_hT1 = Buf("hT1")
                wb1 = [sb(ph, "wb1%d" % i, [128, DC, 512], BF16) for i in range(2)]
                bwb1 = [Buf(), Buf()]
                xb = sb(ph, "xb", [128, D], F32)
                junk = sb(ph, "junk", [128, D], BF16)
                ss = sb(ph, "ss", [128, 4], F32)
                nb = (xb, Buf(), junk, Buf(), ss, Buf())
                cosT = sb(ph, "cosT", [128, T1], F32)
                sinT = sb(ph, "sinT", [128, T1], F32)
                b_cs = Buf()
                bgT = sb(ph, "bgT", [128, 3 * DC], F32)
                load_featmajor(ph, lambda c0, n: bgT[:, c0:c0 + n], W["b_gates"][l], 3 * DC, 7, "bg")
                stb = Ring([(sb(ph, "stb%d" % i, [128, 512], BF16), Buf()) for i in range(4)])
                stf = Ring([(sb(ph, "stf%d" % i, [128, 512], F32), Buf()) for i in range(4)])
                junk2 = sb(ph, "junk2", [128, 512], BF16)
                b_junk2 = Buf()
                mmb = Ring([0, 1, 2, 3])
                rtb = Ring([4, 5])

                groups = []
                win = W["w_in"][l]
                for c0 in range(0, 2 * c.QK, 512):
                    groups.append(("qk", [(c0, 512, 0)], c0))
                for c0 in range(0, c.AW, 512):
                    groups.append(("v", [(c.ov + c0, 512, 0)], c0))
                for c0 in range(0, c.SW, 512):
                    n = min(512, c.SW - c0)
                    groups.append(("u", [(c.ou + c0, n, 0)], c0))
                gw = min(512, c.SW)
                for c0 in range(0, c.SW, gw):
                    groups.append(("sv", [(c.osv + c0, gw, 0)], c0))
                for c0 in range(0, c.CW, 256):
                    groups.append(("conv", [(c.oca + c0, 256, 0), (c.ocb + c0, 256, 256)], c0))
                for c0 in range(0, 3 * D, 512):
                    groups.append(("gate", [(c.og + c0, 512, 0)], c0))
                import os as _os
                _kk = _os.environ.get('KKINDS')
                if _kk is not None:
                    groups = [g_ for g_ in groups if g_[0] in _kk.split(',')]
                    if not groups:
                        groups = [('none', [(0, 512, 0)], 0)]
                seq = [(t, gi) for t in range(NT // T1) for gi in range(len(groups))]

                def loadW(i):
                    t, gi = seq[i]
                    for (col, n, dcol) in groups[gi][1]:
                        S.dma("pool", wb1[i % 2][:, :, dcol:dcol + n],
                              win[:, col:col + n].rearrange("(kc p) n -> p kc n", p=128), writes=[bwb1[i % 2]])

                def fm_mm(i, j, s):
                    bank = mmb.next()

                    def mm(e, bank=bank):
                        for kc in range(DC):
                            ins = e.matmul(ps[bank][:, :], lhsT=wb1[i % 2][:, kc, j * 128:(j + 1) * 128],
                                           rhs=hT1[:, kc, s * 512:(s + 1) * 512], start=(kc == 0),
                                           stop=(kc == DC - 1))
                        return ins
                    S.op("pe", mm, reads=[b_hT1, bwb1[i % 2]], writes=[bps[bank]])
                    return bank

                def tm_mm(i, blk, n):
                    bank = mmb.next()

                    def mm(e, bank=bank):
                        for kc in range(DC):
                            ins = e.matmul(ps[bank][:, 0:n], lhsT=hT1[:, kc, blk * 128:(blk + 1) * 128],
                                           rhs=wb1[i % 2][:, kc, 0:n], start=(kc == 0), stop=(kc == DC - 1))
                        return ins
                    S.op("pe", mm, reads=[b_hT1, bwb1[i % 2]], writes=[bps[bank]])
                    return bank

                loadW(0)
                for i, (t, gi) in enumerate(seq):
                    t0 = t * T1
                    if gi == 0:
                        for blk in range(T1 // 128):
                            norm_to_hT(nb, xcur, t0 + blk * 128, blk, hT1, b_hT1, 0, 0, l, [6, 7])
                        S.dma("sp", cosT[:], k_cos[:, t0:t0 + T1], writes=[b_cs])
                        S.dma("sp", sinT[:], k_sin[:, t0:t0 + T1], writes=[b_cs])
                    if i + 1 < len(seq):
                        loadW(i + 1)
                    kind, pieces, c0 = groups[gi]
                    if kind == "qk":
                        for s in range(NS):
                            ts = t0 + s * 512
                            for j in range(4):
                                ci = c0 // 128 + j
                                dstT = QT[ci] if ci < c.QK // 128 else KT[ci - c.QK // 128]
                                bank = fm_mm(i, j, s)
                                zb, bzb = stb.next()
                                S.op("act", lambda e, zb=zb, bank=bank: e.activation(out=zb[:], in_=ps[bank][:],
                                                                                      func=AF.Copy),
                                     reads=[bps[bank]], writes=[bzb])
                                rb = rtb.next()
                                S.op("pe", lambda e, zb=zb, rb=rb: e.matmul(ps[rb][:], lhsT=pswap[:], rhs=zb[:],
                                                                            start=True, stop=True),
                                     reads=[bzb, bC], writes=[bps[rb]])
                                t1, bt1 = stf.next()
                                S.op("dve", lambda e, t1=t1, bank=bank, s=s: e.tensor_tensor(
                                    out=t1[:], in0=ps[bank][:], in1=cosT[:, s * 512:(s + 1) * 512], op=ALU.mult),
                                    reads=[bps[bank], b_cs], writes=[bt1])
                                t2, bt2 = stf.next()
                                S.op("dve", lambda e, t2=t2, rb=rb, s=s: e.tensor_tensor(
                                    out=t2[:], in0=ps[rb][:], in1=sinT[:, s * 512:(s + 1) * 512], op=ALU.mult),
                                    reads=[bps[rb], b_cs], writes=[bt2])
                                ob, bob = stb.next()
                                S.op("dve", lambda e, ob=ob, t1=t1, t2=t2: e.tensor_tensor(
                                    out=ob[:], in0=t1[:], in1=t2[:], op=ALU.add), reads=[bt1, bt2], writes=[bob])
                                S.dma("sp", dstT[:, ts:ts + 512], ob[:], reads=[bob])
                    elif kind == "u":
                        nj = pieces[0][1] // 128
                        for s in range(NS):
                            ts = t0 + s * 512
                            for j in range(nj):
                                ci = c0 // 128 + j
                                bank = fm_mm(i, j, s)
                                ob, bob = stb.next()
                                S.op("act", lambda e, ob=ob, bank=bank: e.activation(out=ob[:], in_=ps[bank][:],
                                                                                      func=AF.Gelu),
                                     reads=[bps[bank]], writes=[bob])
                                S.dma("sp", UT[ci][:, ts:ts + 512], ob[:], reads=[bob])
                    elif kind == "gate":
                        for s in range(NS):
                            ts = t0 + s * 512
                            for j in range(4):
                                ci = c0 // 128 + j
                                bank = fm_mm(i, j, s)
                                ob, bob = stb.next()
                                S.op("act", lambda e, ob=ob, bank=bank, ci=ci: e.activation(
                                    out=ob[:], in_=ps[bank][:], func=AF.Sigmoid, bias=bgT[:, ci:ci + 1]),
                                    reads=[bps[bank], bC], writes=[bob])
                                S.dma("sp", GT[ci][:, ts:ts + 512], ob[:], reads=[bob])
                    elif kind == "conv":
                        for s in range(NS):
                            ts = t0 + s * 512
                            for j in range(2):
                                ci = c0 // 128 + j
                                ba = fm_mm(i, j, s)
                                bb_ = fm_mm(i, 2 + j, s)
                                sg, bsg = stf.next()
                                S.op("act", lambda e, sg=sg, bb_=bb_: e.activation(out=sg[:], in_=ps[bb_][:],
                                                                                    func=AF.Sigmoid),
                                     reads=[bps[bb_]], writes=[bsg])
                                yo, byo = stf.next()
                                S.op("dve", lambda e, yo=yo, sg=sg, ba=ba: e.tensor_tensor(
                                    out=yo[:], in0=ps[ba][:], in1=sg[:], op=ALU.mult),
                                    reads=[bps[ba], bsg], writes=[byo])
                                S.dma("sp", YT[ci][:, HALO + ts:HALO + ts + 512], yo[:], reads=[byo])
                                if PAIR and ts + 512 == NT:
                                    S.dma("sp", YTAIL.rearrange("c (p k) -> c p k", k=32)[ci], yo[:, 480:512],
                                          reads=[byo])
                    elif kind == "v":
                        for blk in range(T1 // 128):
                            r0 = t0 + blk * 128
                            bank = tm_mm(i, blk, 512)
                            ob, bob = stb.next()
                            S.op("dve", lambda e, ob=ob, bank=bank: e.tensor_copy(out=ob[:], in_=ps[bank][:]),
                                 reads=[bps[bank]], writes=[bob])
                            S.dma("sp", VV[r0:r0 + 128, c0:c0 + 512], ob[:], reads=[bob])
                    elif kind == "sv":
                        g_i = c0 // gw
                        for blk in range(T1 // 128):
                            r0 = t0 + blk * 128
                            ba = r0 // 128
                            bank = tm_mm(i, blk, gw)
                            of, bof = stf.next()
                            S.op("act", lambda e, of=of, bank=bank, ba=ba, g_i=g_i: e.activation(
                                out=of[:, 0:gw], in_=ps[bank][:, 0:gw], func=AF.Gelu,
                                accum_out=stat1[:, ba, g_i:g_i + 1]), reads=[bps[bank]], writes=[bof, b_stat])
                            S.op("act", lambda e, of=of, ba=ba, g_i=g_i: e.activation(
                                out=junk2[:, 0:gw], in_=of[:, 0:gw], func=AF.Square,
                                accum_out=stat2[:, ba, g_i:g_i + 1]), reads=[bof], writes=[b_junk2, b_stat])
                            S.dma("sp", SVR[r0:r0 + 128, c0:c0 + gw], of[:, 0:gw], reads=[bof])
                S.barrier()

            b_ktg, b_vvg, b_ytg = Buf("ktg"), Buf("vvg"), Buf("ytg")
            if PAIR:
                with ExitStack() as ph:
                    kt2d = KT.rearrange("k p t -> (k p) t")
                    for p_ in range(c.QK // PR):
                        S.cc(kt2d[p_ * PR:(p_ + 1) * PR, :], KTG[p_], RG, writes=[b_ktg])
                    for p_ in range(NT // PR):
                        S.cc(VV[p_ * PR:(p_ + 1) * PR, :], VVG[p_], RG, writes=[b_vvg])
                    S.cc(YTAIL, YTG, RG, writes=[b_ytg])
                    yh = Ring([(sb(ph, "yh%d" % i, [128, 32], F32), Buf()) for i in range(2)])
                    for cc_ in range(c.CC):
                        t_, bt_ = yh.next()
                        S.dma("sp", t_[:], YTG.rearrange("c (p k) -> c p k", k=32)[cc_], reads=[b_ytg], writes=[bt_])
                        S.op("dve", lambda e, t_=t_: e.tensor_scalar(out=t_[:], in0=t_[:], scalar1=flag[:, 0:1],
                                                                     scalar2=None, op0=ALU.mult),
                             reads=[bt_, bC], writes=[bt_])
                        S.dma("sp", YT[cc_][:, 0:HALO], t_[:], reads=[bt_])
                    S.barrier()
            if c.stop == 'P1':
                return True
            with ExitStack() as ph:
                NG = NT // 512
                scale = 128.0 ** -0.5
                NK = NRC * 128 + NT
                NKS = NK // 512
                kt = [sb(ph, "kt%d" % i, [128, 2, NK], BF16) for i in range(2)]
                v1 = [sb(ph, "v1%d" % i, [128, NK // 128, 257], BF16) for i in range(2)]
                bkt = [Buf(), Buf()]
                bv1 = [Buf(), Buf()]
                qt = [sb(ph, "qtt%d" % i, [128, 2, 512], BF16) for i in range(2)]
                bqt = [Buf(), Buf()]
                sq = Ring([(sb(ph, "sq%d" % i, [128, 512], BF16), Buf()) for i in range(2)])
                pT = Ring([(sb(ph, "pT%d" % i, [128, 512], BF16), Buf()) for i in range(3)])
                kmx = sb(ph, "kmx", [128, 2, NKS + 2], F32)
                b_kmx = Buf()
                negm = [sb(ph, "negm%d" % i, [128, 512], BF16) for i in range(2)]
                bnegm = [Buf(), Buf()]
                on0 = sb(ph, "on0", [128, 4, 256], F32)
                b_on0 = Buf()
                od = Ring([(sb(ph, "od%d" % i, [128, 256], F32), Buf()) for i in range(2)])
                rl = sb(ph, "rl", [128, 8], F32)
                b_rl = Buf()
                junk3 = sb(ph, "junk3", [128, 256], BF16)
                b_junk3 = Buf()
                oT = Ring([(sb(ph, "oT%d" % i, [128, 2, 512], BF16), Buf()) for i in range(2)])
                gco = sb(ph, "gco", [128, 256], F32)
                lqb = sb(ph, "lqb", [128, 512], F32)
                lw = sb(ph, "lw", [128, 8], F32)
                b_lam = Buf()
                S.dma("sp", lqb[:], W["lam_qk"][l].rearrange("a b -> (a b)").partition_broadcast(128),
                      writes=[b_lam])
                S.dma("sp", gco[:], W["attn_norm_g"][l].partition_broadcast(128), writes=[b_lam])
                S.op("dve", lambda e: e.tensor_tensor(out=lqb[:, 0:128], in0=lqb[:, 0:128], in1=lqb[:, 128:256],
                                                      op=ALU.mult), reads=[b_lam], writes=[b_lam])
                S.op("dve", lambda e: e.tensor_tensor(out=lqb[:, 256:384], in0=lqb[:, 256:384],
                                                      in1=lqb[:, 384:512], op=ALU.mult),
                     reads=[b_lam], writes=[b_lam])
                S.op("dve", lambda e: e.reduce_sum(out=lw[:, 0:1], in_=lqb[:, 0:128], axis=AX.X),
                     reads=[b_lam], writes=[b_lam])
                S.op("dve", lambda e: e.reduce_sum(out=lw[:, 1:2], in_=lqb[:, 256:384], axis=AX.X),
                     reads=[b_lam], writes=[b_lam])
                S.op("act", lambda e: e.activation(out=lw[:, 2:4], in_=lw[:, 0:2], func=AF.Exp),
                     reads=[b_lam], writes=[b_lam])
                S.op("dve", lambda e: e.tensor_tensor(out=lw[:, 4:5], in0=lw[:, 3:4], in1=lw[:, 2:3],
                                                      op=ALU.subtract), reads=[b_lam], writes=[b_lam])
                S.op("dve", lambda e: e.tensor_scalar(out=lw[:, 4:5], in0=lw[:, 4:5], scalar1=-lam_init,
                                                      scalar2=None, op0=ALU.add), reads=[b_lam], writes=[b_lam])
                S.op("dve", lambda e: e.tensor_scalar(out=gco[:], in0=gco[:], scalar1=(1.0 - lam_init),
                                                      scalar2=None, op0=ALU.mult), reads=[b_lam], writes=[b_lam])
                for i in range(2):
                    S.op("dve", lambda e, i=i: e.memset(v1[i][:, :, 256:257], 1.0), writes=[bv1[i]])
                    if PAIR:
                        S.op("dve", lambda e, i=i: e.tensor_scalar(
                            out=v1[i][:, 0:NRC, 256:257], in0=v1[i][:, 0:NRC, 256:257], scalar1=flag[:, 0:1],
                            scalar2=None, op0=ALU.mult), reads=[bC], writes=[bv1[i]])
                SB = [0, 1]
                OB = [2, 3, 4, 5]
                MB = 6
                TB = 7
                ada_next = make_ada(l + 1, ph, MB) if l + 1 < L else (lambda: False)
                kmxs = [kmx, sb(ph, "kmx1", [128, 2, NKS + 2], F32)]
                b_kmxs = [b_kmx, Buf()]
                rl0 = sb(ph, "rl0", [128, 4], F32)
                rl1 = sb(ph, "rl1", [128, 4], F32)
                b_rl0, b_rl1 = Buf(), Buf()
                rts = Ring([(sb(ph, "rts%d" % i, [128, 4], F32), Buf()) for i in range(2)])
                ods = Ring([(sb(ph, "ods%d" % i, [128, 256], F32), Buf()) for i in range(4)])

                def head_setup(h):
                    kb, vb = kt[h % 2], v1[h % 2]
                    km, bkm = kmxs[h % 2], b_kmxs[h % 2]
                    for m in range(2):
                        S.dma("sp", kb[:, m, NRC * 128:NK], KT[2 * h + m], writes=[bkt[h % 2]])
                        if PAIR:
                            r_ = (2 * h + m) * 128
                            S.dma("sp", kb[:, m, 0:NRC * 128], KTG[r_ // PR][r_ % PR:r_ % PR + 128, :],
                                  reads=[b_ktg], writes=[bkt[h % 2]])
                    S.dma("sp", vb[:, NRC:NK // 128, 0:256],
                          VV[:, h * 256:(h + 1) * 256].rearrange("(j p) v -> p j v", p=128), writes=[bv1[h % 2]])
                    if PAIR:
                        for p_ in range(NT // PR):
                            S.dma("sp", vb[:, p_ * (PR // 128):(p_ + 1) * (PR // 128), 0:256],
                                  VVG[p_][0:PR, h * 256:(h + 1) * 256].rearrange("(j p) v -> p j v", p=128),
                                  reads=[b_vvg], writes=[bv1[h % 2]])
                        S.op("dve", lambda e, vb=vb: e.tensor_scalar(
                            out=vb[:, 0:NRC, 0:256], in0=vb[:, 0:NRC, 0:256], scalar1=flag[:, 0:1], scalar2=None,
                            op0=ALU.mult), reads=[bC], writes=[bv1[h % 2]])
                    for m in range(2):
                        for s in range(NKS):
                            sqt, bsq = sq.next()
                            S.op("dve", lambda e, sqt=sqt, m=m, s=s, kb=kb: e.tensor_tensor(
                                out=sqt[:], in0=kb[:, m, s * 512:(s + 1) * 512],
                                in1=kb[:, m, s * 512:(s + 1) * 512], op=ALU.mult),
                                reads=[bkt[h % 2]], writes=[bsq])
                            S.op("pe", lambda e, sqt=sqt: e.matmul(ps[MB][:, :], lhsT=ones_b[:, :], rhs=sqt[:],
                                                                    start=True, stop=True),
                                 reads=[bsq, bC], writes=[bps[MB]])
                            S.op("dve", lambda e, m=m, s=s, km=km: e.reduce_max(out=km[:, m, s:s + 1],
                                                                                 in_=ps[MB][:, :], axis=AX.X),
                                 reads=[bps[MB]], writes=[bkm])
                        S.op("dve", lambda e, m=m, km=km: e.reduce_max(out=km[:, m, NKS:NKS + 1], in_=km[:, m, 0:NKS],
                                                                       axis=AX.X), reads=[bkm], writes=[bkm])
                        S.op("dve", lambda e, m=m, km=km: e.tensor_scalar(
                            out=km[:, m, NKS + 1:NKS + 2], in0=km[:, m, NKS:NKS + 1], scalar1=-0.5 / 128.0,
                            scalar2=None, op0=ALU.mult), reads=[bkm], writes=[bkm])

                gstate = {}

                def group_setup(h, G):
                    hq = h * NG + G
                    qb, bq = qt[hq % 2], bqt[hq % 2]
                    for m in range(2):
                        S.dma("sp", qb[:, m, :], QT[2 * h + m][:, G * 512:G * 512 + 512], writes=[bq])
                    oTt, boT = oT.next()
                    gstate[(h, G)] = (qb, bq, oTt, boT)

                def prologue(h, G, m):
                    qb, bq, oTt, boT = gstate[(h, G)]
                    km, bkm = kmxs[h % 2], b_kmxs[h % 2]
                    sqt, bsq = sq.next()
                    S.op("dve", lambda e, sqt=sqt, m=m, qb=qb: e.tensor_tensor(
                        out=sqt[:], in0=qb[:, m, :], in1=qb[:, m, :], op=ALU.mult), reads=[bq], writes=[bsq])
                    S.op("pe", lambda e, sqt=sqt: e.matmul(ps[MB][:, :], lhsT=ones_b[:, :], rhs=sqt[:],
                                                            start=True, stop=True),
                         reads=[bsq, bC], writes=[bps[MB]])
                    nm = negm[m]
                    S.op("dve", lambda e, nm=nm, m=m, km=km: e.tensor_scalar(
                        out=nm[:], in0=ps[MB][:, :], scalar1=-0.5 / 128.0, scalar2=km[:, m, NKS + 1:NKS + 2],
                        op0=ALU.mult, op1=ALU.add), reads=[bps[MB], bkm], writes=[bnegm[m]])

                def mainloop(h, G, m):
                    qb, bq, oTt, boT = gstate[(h, G)]
                    kb, vb = kt[h % 2], v1[h % 2]
                    nm = negm[m]
                    jmax = NRC + 4 * (G + 1)

                    def qk(j):
                        jb = max(0, j - NRC - 4 * G)
                        bank = SB[j % 2]
                        cols = slice(jb * 128, 512)

                        def f(e):
                            e.matmul(ps[bank][:, cols], lhsT=kb[:, m, j * 128:(j + 1) * 128],
                                     rhs=qb[:, m, cols], start=True, stop=False)
                            return e.matmul(ps[bank][:, cols], lhsT=ones_b[:, :], rhs=nm[:, cols],
                                            start=False, stop=True)
                        S.op("pe", f, reads=[bkt[h % 2], bq, bnegm[m], bC], writes=[bps[bank]])
                    qk(0)
                    for j in range(jmax):
                        if j + 1 < jmax:
                            qk(j + 1)
                        jb = max(0, j - NRC - 4 * G)
                        bank = SB[j % 2]
                        cols = slice(jb * 128, 512)
                        pt, bpt = pT.next()
                        S.op("act", lambda e, pt=pt, bank=bank, cols=cols: e.activation(
                            out=pt[:, cols], in_=ps[bank][:, cols], func=AF.Exp, scale=scale),
                            reads=[bps[bank]], writes=[bpt])
                        if j >= NRC + 4 * G:
                            dc = slice(jb * 128, (jb + 1) * 128)
                            S.op("dve", lambda e, pt=pt, dc=dc: e.tensor_tensor(
                                out=pt[:, dc], in0=pt[:, dc], in1=tri[:], op=ALU.mult),
                                reads=[bpt, bC], writes=[bpt])

                        def av(e, pt=pt, j=j, jb=jb):
                            ins = None
                            for i in range(jb, 4):
                                ins = e.matmul(ps[OB[i]][:, 0:257], lhsT=pt[:, i * 128:(i + 1) * 128],
                                               rhs=vb[:, j, :], start=(j == 0), stop=(j == NRC + 4 * G + i))
                            return ins
                        S.op("pe", av, reads=[bpt, bv1[h % 2]], writes=[bps[OB[i]] for i in range(jb, 4)])

                def evac(h, G, m):
                    outs = []
                    for i in range(4):
                        if m == 0:
                            S.op("dve", lambda e, i=i: e.reciprocal(out=rl0[:, i:i + 1], in_=ps[OB[i]][:, 256:257]),
                                 reads=[bps[OB[i]]], writes=[b_rl0])
                            S.op("dve", lambda e, i=i: e.tensor_scalar(
                                out=on0[:, i, :], in0=ps[OB[i]][:, 0:256], scalar1=rl0[:, i:i + 1],
                                scalar2=None, op0=ALU.mult), reads=[bps[OB[i]], b_rl0], writes=[b_on0])
                        else:
                            S.op("dve", lambda e, i=i: e.reciprocal(out=rl1[:, i:i + 1], in_=ps[OB[i]][:, 256:257]),
                                 reads=[bps[OB[i]]], writes=[b_rl1])
                            S.op("dve", lambda e, i=i: e.tensor_scalar(
                                out=rl1[:, i:i + 1], in0=rl1[:, i:i + 1], scalar1=lw[:, 4:5],
                                scalar2=None, op0=ALU.mult), reads=[b_rl1, b_lam], writes=[b_rl1])
                            odt, bod = ods.next()
                            S.op("dve", lambda e, i=i, odt=odt: e.scalar_tensor_tensor(
                                out=odt[:], in0=ps[OB[i]][:, 0:256], scalar=rl1[:, i:i + 1],
                                in1=on0[:, i, :], op0=ALU.mult, op1=ALU.add),
                                reads=[bps[OB[i]], b_rl1, b_on0], writes=[bod])
                            outs.append((odt, bod))
                    return outs

                def finish(h, G, outs):
                    qb, bq, oTt, boT = gstate.pop((h, G))
                    for i, (odt, bod) in enumerate(outs):
                        rt, brt = rts.next()
                        S.op("act", lambda e, odt=odt, rt=rt: e.activation(
                            out=junk3[:], in_=odt[:], func=AF.Square, accum_out=rt[:, 0:1]),
                            reads=[bod], writes=[b_junk3, brt])
                        S.op("act", lambda e, rt=rt: e.activation(out=rt[:, 1:2], in_=rt[:, 0:1], func=AF.Sqrt,
                                                                  scale=1.0 / 256, bias=1e-6),
                             reads=[brt], writes=[brt])
                        S.op("dve", lambda e, rt=rt: e.reciprocal(out=rt[:, 2:3], in_=rt[:, 1:2]),
                             reads=[brt], writes=[brt])
                        S.op("dve", lambda e, odt=odt, rt=rt: e.scalar_tensor_tensor(
                            out=odt[:], in0=odt[:], scalar=rt[:, 2:3], in1=gco[:], op0=ALU.mult,
                            op1=ALU.mult), reads=[bod, brt, b_lam], writes=[bod])

                        def tr2(e, odt=odt):
                            e.transpose(out=ps[TB][:, 0:128], in_=odt[:, 0:128], identity=ident[:])
                            return e.transpose(out=ps[TB][:, 128:256], in_=odt[:, 128:256], identity=ident[:])
                        S.op("pe", tr2, reads=[bod, bC], writes=[bps[TB]])
                        S.op("act", lambda e, i=i, oTt=oTt: e.activation(
                            out=oTt[:, :, i * 128:(i + 1) * 128],
                            in_=ps[TB][:, 0:256].rearrange("p (a t) -> p a t", a=2), func=AF.Copy),
                            reads=[bps[TB]], writes=[boT])
                    for a in range(2):
                        S.dma("sp", OAT[2 * h + a][:, G * 512:G * 512 + 512], oTt[:, a, :], reads=[boT])

                combos = [(h, G, m) for h in range(c.H) for G in range(NG) for m in range(2)]

                def setup_for(idx):
                    h, G, m = combos[idx]
                    if m == 0:
                        if G == 0:
                            head_setup(h)
                        group_setup(h, G)
                    prologue(h, G, m)
                setup_for(0)
                for idx, (h, G, m) in enumerate(combos):
                    mainloop(h, G, m)
                    if idx + 1 < len(combos):
                        setup_for(idx + 1)
                    outs = evac(h, G, m)
                    if m == 1:
                        finish(h, G, outs)
                        ada_next()
                        ada_next()
                while ada_next():
                    pass
                S.barrier()

            if c.stop == 'P3':
                return True
            with ExitStack() as ph:
                SW, SG, CC, CK = c.SW, c.SG, c.CC, c.CK
                wsT = sb(ph, "wsT", [128, SG, 128], BF16)
                wtmp = sb(ph, "wtmp", [128, SG, 128], F32)
                sbb = sb(ph, "sbb", [128, SG, 4, 128], F32)
                lng = sb(ph, "lng", [128, SW], F32)
                lnb = sb(ph, "lnb", [128, SW], F32)
                bP = Buf("p4c")
                S.dma("sp", wtmp[:], W["sgu_w"][l].rearrange("g t s -> t g s"), writes=[bP])
                for a in range(4):
                    S.dma("sp", sbb[:, :, a, :], W["sgu_b"][l].partition_broadcast(128), writes=[bP])
                S.dma("sp", lng[:], W["sgu_ln_g"][l].partition_broadcast(128), writes=[bP])
                S.dma("sp", lnb[:], W["sgu_ln_b"][l].partition_broadcast(128), writes=[bP])
                for g in range(SG):
                    S.op("pe", lambda e, g=g: e.transpose(out=ps[7][:, 0:128], in_=wtmp[:, g, :], identity=ident[:]),
                         reads=[bP, bC], writes=[bps[7]])
                    S.op("dve", lambda e, g=g: e.tensor_tensor(out=wsT[:, g, :], in0=ps[7][:, 0:128], in1=tri[:],
                                                               op=ALU.mult), reads=[bps[7], bC], writes=[bP])
                cw = sb(ph, "cw", [128, CC, CK], F32)
                cwt = sb(ph, "cwt", [CK, c.CW], F32)
                S.dma("sp", cwt[:], W["conv_dw_w"][l], writes=[bP])
                for cc in range(CC):
                    S.op("pe", lambda e, cc=cc: e.transpose(out=ps[7][:, 0:CK], in_=cwt[0:CK, cc * 128:(cc + 1) * 128],
                                                            identity=ident[0:CK, 0:CK]),
                         reads=[bP, bC], writes=[bps[7]])
                    S.op("dve", lambda e, cc=cc: e.tensor_copy(out=cw[:, cc, :], in_=ps[7][:, 0:CK]),
                         reads=[bps[7]], writes=[bP])
                cvb = sb(ph, "cvb", [128, CC], F32)
                clg = sb(ph, "clg", [128, CC], F32)
                clb = sb(ph, "clb", [128, CC], F32)
                load_featmajor(ph, lambda c0, n: cvb[:, c0:c0 + n], W["conv_dw_b"][l], CC, 7, "cvb")
                load_featmajor(ph, lambda c0, n: clg[:, c0:c0 + n], W["conv_ln_g"][l], CC, 7, "clg")
                load_featmajor(ph, lambda c0, n: clb[:, c0:c0 + n], W["conv_ln_b"][l], CC, 7, "clb")

                svr = Ring([(sb(ph, "svr%d" % i, [128, SW], F32), Buf()) for i in range(2)])
                svn = Ring([(sb(ph, "svn%d" % i, [128, 4, SW], BF16), Buf()) for i in range(2)])
                utt = Ring([(sb(ph, "utt%d" % i, [128, SG, 512], BF16), Buf()) for i in range(2)])
                mst = sb(ph, "mst", [128, 8], F32)
                b_mst = Buf()
                tmpf = Ring([(sb(ph, "tmpf%d" % i, [128, 512], F32), Buf()) for i in range(3)])
                osb = Ring([(sb(ph, "osb%d" % i, [128, 512], BF16), Buf()) for i in range(3)])
                yin = Ring([(sb(ph, "yin%d" % i, [128, HALO + 512], F32), Buf()) for i in range(2)])
                acc = sb(ph, "acc", [128, CC, 2, 512], F32)
                b_acc = [Buf() for _ in range(CC)]
                sqa = Ring([(sb(ph, "sqa%d" % i, [128, 512], F32), Buf()) for i in range(2)])
                mrs = sb(ph, "mrs", [128, 4, 512], F32)
                b_mrs = Buf()
                for t in range(NT // 512):
                    t0 = t * 512
                    ut, but = utt.next()
                    S.dma("sp", ut[:], UT[:, :, t0:t0 + 512].rearrange("g p t -> p g t"), writes=[but])
                    sv, bsv = svn.next()
                    for n in range(4):
                        ba = t0 // 128 + n
                        sr, bsr = svr.next()
                        S.dma("sp", sr[:], SVR[t0 + n * 128:t0 + (n + 1) * 128, :], writes=[bsr])
                        ngr = (SW + 511) // 512
                        if ngr == 2:
                            S.op("dve", lambda e, ba=ba: e.tensor_tensor(out=mst[:, 0:1], in0=stat1[:, ba, 0:1],
                                                                         in1=stat1[:, ba, 1:2], op=ALU.add),
                                 reads=[b_stat], writes=[b_mst])
                            S.op("dve", lambda e, ba=ba: e.tensor_tensor(out=mst[:, 1:2], in0=stat2[:, ba, 0:1],
                                                                         in1=stat2[:, ba, 1:2], op=ALU.add),
                                 reads=[b_stat], writes=[b_mst])
                        else:
                            S.op("dve", lambda e, ba=ba: e.tensor_copy(out=mst[:, 0:1], in_=stat1[:, ba, 0:1]),
                                 reads=[b_stat], writes=[b_mst])
                            S.op("dve", lambda e, ba=ba: e.tensor_copy(out=mst[:, 1:2], in_=stat2[:, ba, 0:1]),
                                 reads=[b_stat], writes=[b_mst])
                        S.op("dve", lambda e: e.tensor_scalar(out=mst[:, 0:2], in0=mst[:, 0:2], scalar1=1.0 / SW,
                                                              scalar2=None, op0=ALU.mult),
                             reads=[b_mst], writes=[b_mst])
                        S.op("dve", lambda e: e.tensor_tensor(out=mst[:, 2:3], in0=mst[:, 0:1], in1=mst[:, 0:1],
                                                              op=ALU.mult), reads=[b_mst], writes=[b_mst])
                        S.op("dve", lambda e: e.tensor_tensor(out=mst[:, 3:4], in0=mst[:, 1:2], in1=mst[:, 2:3],
                                                              op=ALU.subtract), reads=[b_mst], writes=[b_mst])
                        S.op("act", lambda e: e.activation(out=mst[:, 4:5], in_=mst[:, 3:4], func=AF.Sqrt,
                                                           bias=1e-5), reads=[b_mst], writes=[b_mst])
                        S.op("dve", lambda e: e.reciprocal(out=mst[:, 5:6], in_=mst[:, 4:5]),
                             reads=[b_mst], writes=[b_mst])
                        S.op("dve", lambda e, sr=sr: e.tensor_scalar(out=sr[:], in0=sr[:], scalar1=mst[:, 0:1],
                                                                     scalar2=mst[:, 5:6], op0=ALU.subtract,
                                                                     op1=ALU.mult),
                             reads=[bsr, b_mst], writes=[bsr])
                        S.op("dve", lambda e, sr=sr: e.tensor_tensor(out=sr[:], in0=sr[:], in1=lng[:], op=ALU.mult),
                             reads=[bsr, bP], writes=[bsr])
                        S.op("dve", lambda e, sr=sr, sv=sv, n=n: e.tensor_tensor(out=sv[:, n, :], in0=sr[:],
                                                                                 in1=lnb[:], op=ALU.add),
                             reads=[bsr, bP], writes=[bsv])
                    for g in range(SG):
                        bank = g % 2

                        def mm(e, g=g, bank=bank, sv=sv):
                            for n in range(4):
                                ins = e.matmul(ps[bank][:, n * 128:(n + 1) * 128],
                                               lhsT=sv[:, n, g * 128:(g + 1) * 128], rhs=wsT[:, g, :],
                                               start=True, stop=True)
                            return ins
                        S.op("pe", mm, reads=[bsv, bP], writes=[bps[bank]])
                        tf, btf = tmpf.next()
                        S.op("dve", lambda e, tf=tf, bank=bank, g=g: e.tensor_tensor(
                            out=tf[:], in0=ps[bank][:], in1=sbb[:, g, :, :].rearrange("p a t -> p (a t)"),
                            op=ALU.add), reads=[bps[bank], bP], writes=[btf])
                        ob, bob = osb.next()
                        S.op("dve", lambda e, ob=ob, tf=tf, ut=ut, g=g: e.tensor_tensor(
                            out=ob[:], in0=tf[:], in1=ut[:, g, :], op=ALU.mult), reads=[btf, but], writes=[bob])
                        S.dma("sp", OST[g][:, t0:t0 + 512], ob[:], reads=[bob])
                    for cc in range(CC):
                        yi, byi = yin.next()
                        S.dma("sp", yi[:], YT[cc][:, t0:t0 + HALO + 512], writes=[byi])
                        off = HALO - (CK - 1)
                        S.op("dve", lambda e, yi=yi, cc=cc, off=off: e.tensor_scalar(
                            out=acc[:, cc, 0, :], in0=yi[:, off:off + 512], scalar1=cw[:, cc, 0:1],
                            scalar2=cvb[:, cc:cc + 1], op0=ALU.mult, op1=ALU.add),
                            reads=[byi, bP], writes=[b_acc[cc]])
                        for j in range(1, CK):
                            S.op("dve", lambda e, yi=yi, cc=cc, off=off, j=j: e.scalar_tensor_tensor(
                                out=acc[:, cc, j % 2, :], in0=yi[:, off + j:off + j + 512],
                                scalar=cw[:, cc, j:j + 1], in1=acc[:, cc, (j - 1) % 2, :], op0=ALU.mult,
                                op1=ALU.add), reads=[byi, bP, b_acc[cc]], writes=[b_acc[cc]])
                        fin = (CK - 1) % 2
                        sqt, bsq = sqa.next()
                        S.op("act", lambda e, sqt=sqt, cc=cc, fin=fin: e.activation(
                            out=sqt[:], in_=acc[:, cc, fin, :], func=AF.Square), reads=[b_acc[cc]], writes=[bsq])
                        S.op("pe", lambda e, cc=cc, fin=fin: e.matmul(ps[2][:], lhsT=ones_f[:], rhs=acc[:, cc, fin, :],
                                                                      start=(cc == 0), stop=(cc == CC - 1)),
                             reads=[b_acc[cc], bC], writes=[bps[2]])
                        S.op("pe", lambda e, cc=cc, sqt=sqt: e.matmul(ps[3][:], lhsT=ones_f[:], rhs=sqt[:],
                                                                      start=(cc == 0), stop=(cc == CC - 1)),
                             reads=[bsq, bC], writes=[bps[3]])
                    fin = (CK - 1) % 2
                    S.op("dve", lambda e: e.tensor_scalar(out=mrs[:, 0, :], in0=ps[2][:], scalar1=1.0 / c.CW,
                                                          scalar2=None, op0=ALU.mult),
                         reads=[bps[2]], writes=[b_mrs])
                    S.op("dve", lambda e: e.tensor_scalar(out=mrs[:, 1, :], in0=ps[3][:], scalar1=1.0 / c.CW,
                                                          scalar2=None, op0=ALU.mult),
                         reads=[bps[3]], writes=[b_mrs])
                    S.op("dve", lambda e: e.tensor_tensor(out=mrs[:, 2, :], in0=mrs[:, 0, :], in1=mrs[:, 0, :],
                                                          op=ALU.mult), reads=[b_mrs], writes=[b_mrs])
                    S.op("dve", lambda e: e.tensor_tensor(out=mrs[:, 1, :], in0=mrs[:, 1, :], in1=mrs[:, 2, :],
                                                          op=ALU.subtract), reads=[b_mrs], writes=[b_mrs])
                    S.op("act", lambda e: e.activation(out=mrs[:, 2, :], in_=mrs[:, 1, :], func=AF.Sqrt, bias=1e-5),
                         reads=[b_mrs], writes=[b_mrs])
                    S.op("dve", lambda e: e.reciprocal(out=mrs[:, 3, :], in_=mrs[:, 2, :]),
                         reads=[b_mrs], writes=[b_mrs])
                    for cc in range(CC):
                        tf, btf = tmpf.next()
                        S.op("dve", lambda e, tf=tf, cc=cc, fin=fin: e.tensor_tensor(
                            out=tf[:], in0=acc[:, cc, fin, :], in1=mrs[:, 0, :], op=ALU.subtract),
                            reads=[b_acc[cc], b_mrs], writes=[btf])
                        S.op("dve", lambda e, tf=tf: e.tensor_tensor(out=tf[:], in0=tf[:], in1=mrs[:, 3, :],
                                                                     op=ALU.mult), reads=[btf, b_mrs], writes=[btf])
                        ob, bob = osb.next()
                        S.op("act", lambda e, ob=ob, tf=tf, cc=cc: e.activation(
                            out=ob[:], in_=tf[:], func=AF.Silu, scale=clg[:, cc:cc + 1], bias=clb[:, cc:cc + 1]),
                            reads=[btf, bC], writes=[bob])
                        S.dma("sp", OCT[cc][:, t0:t0 + 512], ob[:], reads=[bob])
                S.barrier()

            if c.stop == 'P4':
                return True
            with ExitStack() as ph:
                AC, SG, CC = c.AW // 128, c.SG, c.CC
                KC5 = AC + SG + CC
                assert KC5 == DC
                bT = sb(ph, "bT", [128, DC, 512], BF16)
                b_bT = Buf()
                mT = sb(ph, "mT", [128, DC, 512], BF16)
                b_mT = Buf()
                wb5 = [sb(ph, "w5%d" % i, [128, DC, 512], BF16) for i in range(2)]
                bwb5 = [Buf(), Buf()]
                g1b = sb(ph, "g1b", [128, D], F32)
                b_g1b = Buf()
                S.dma("sp", g1b[:], MODS[l, 2 * D:3 * D].partition_broadcast(128), writes=[b_g1b])
                gts = Ring([(sb(ph, "gts%d" % i, [128, 3, 512], BF16), Buf()) for i in range(2)])
                mf = Ring([(sb(ph, "mf%d" % i, [128, 512], F32), Buf()) for i in range(2)])
                tf5 = Ring([(sb(ph, "tf5%d" % i, [128, 512], F32), Buf()) for i in range(2)])
                xp = Ring([(sb(ph, "xp%d" % i, [128, 512], F32), Buf()) for i in range(3)])
                NTL = NT // 512
                seq = [(t, ph_, g) for t in range(NTL) for ph_ in (0, 1) for g in range(D // 512)]

                def loadW(i):
                    t, ph_, g = seq[i]
                    cs = slice(g * 512, (g + 1) * 512)
                    buf = wb5[i % 2]
                    if ph_ == 0:
                        for (src, k0, nk) in ((W["w_proj_attn"][l], 0, AC), (W["w_proj_sgu"][l], AC, SG),
                                              (W["w_proj_conv"][l], AC + SG, CC)):
                            S.dma("pool", buf[:, k0:k0 + nk, :], src[:, cs].rearrange("(kc p) n -> p kc n", p=128),
                                  writes=[bwb5[i % 2]])
                    else:
                        S.dma("pool", buf[:], W["w_out"][l][:, cs].rearrange("(kc p) n -> p kc n", p=128),
                              writes=[bwb5[i % 2]])
                loadW(0)
                mm5 = Ring([0, 1, 2, 3, 4, 5])
                for i, (t, ph_, g) in enumerate(seq):
                    t0 = t * 512
                    if ph_ == 0 and g == 0:
                        S.dma("sp", bT[:, 0:AC, :], OAT[:, :, t0:t0 + 512].rearrange("k p t -> p k t"), writes=[b_bT])
                        S.dma("sp", bT[:, AC:AC + SG, :], OST[:, :, t0:t0 + 512].rearrange("k p t -> p k t"),
                              writes=[b_bT])
                        S.dma("sp", bT[:, AC + SG:DC, :], OCT[:, :, t0:t0 + 512].rearrange("k p t -> p k t"),
                              writes=[b_bT])
                    if i + 1 < len(seq):
                        loadW(i + 1)
                    if ph_ == 0:
                        for j in range(4):
                            ci = g * 4 + j
                            gt, bgt = gts.next()
                            S.dma("sp", gt[:], GT[:, :, t0:t0 + 512].rearrange("(b c) p t -> c p b t", b=3)[ci],
                                  writes=[bgt])
                            banks = []
                            for (k0, nk) in ((0, AC), (AC, SG), (AC + SG, CC)):
                                bank = mm5.next()
                                banks.append(bank)

                                def mm(e, bank=bank, k0=k0, nk=nk, j=j, i=i):
                                    for kc in range(k0, k0 + nk):
                                        ins = e.matmul(ps[bank][:], lhsT=wb5[i % 2][:, kc, j * 128:(j + 1) * 128],
                                                       rhs=bT[:, kc, :], start=(kc == k0), stop=(kc == k0 + nk - 1))
                                    return ins
                                S.op("pe", mm, reads=[b_bT, bwb5[i % 2]], writes=[bps[bank]])
                            m_, bm_ = mf.next()
                            S.op("dve", lambda e, m_=m_, gt=gt, b0=banks[0]: e.tensor_tensor(
                                out=m_[:], in0=ps[b0][:], in1=gt[:, 0, :], op=ALU.mult),
                                reads=[bps[banks[0]], bgt], writes=[bm_])
                            for bi in (1, 2):
                                tt, btt = tf5.next()
                                S.op("dve", lambda e, tt=tt, gt=gt, bk=banks[bi], bi=bi: e.tensor_tensor(
                                    out=tt[:], in0=ps[bk][:], in1=gt[:, bi, :], op=ALU.mult),
                                    reads=[bps[banks[bi]], bgt], writes=[btt])
                                if bi == 1:
                                    S.op("dve", lambda e, m_=m_, tt=tt: e.tensor_tensor(
                                        out=m_[:], in0=m_[:], in1=tt[:], op=ALU.add), reads=[bm_, btt], writes=[bm_])
                                else:
                                    S.op("dve", lambda e, m_=m_, tt=tt, ci=ci: e.tensor_tensor(
                                        out=mT[:, ci, :], in0=m_[:], in1=tt[:], op=ALU.add),
                                        reads=[bm_, btt], writes=[b_mT])
                    else:
                        for blk in range(4):
                            r0 = t0 + blk * 128
                            bank = mm5.next()

                            def mm(e, bank=bank, blk=blk, i=i):
                                for kc in range(DC):
                                    ins = e.matmul(ps[bank][:], lhsT=mT[:, kc, blk * 128:(blk + 1) * 128],
                                                   rhs=wb5[i % 2][:, kc, :], start=(kc == 0), stop=(kc == DC - 1))
                                return ins
                            S.op("pe", mm, reads=[b_mT, bwb5[i % 2]], writes=[bps[bank]])
                            xt_, bxt = xp.next()
                            S.dma("sp", xt_[:], xcur[r0:r0 + 128, g * 512:(g + 1) * 512], writes=[bxt])
                            tt, btt = tf5.next()
                            S.op("dve", lambda e, tt=tt, bank=bank, g=g: e.tensor_tensor(
                                out=tt[:], in0=ps[bank][:], in1=g1b[:, g * 512:(g + 1) * 512], op=ALU.mult),
                                reads=[bps[bank], b_g1b], writes=[btt])
                            S.op("dve", lambda e, tt=tt, xt_=xt_: e.tensor_tensor(
                                out=xt_[:], in0=xt_[:], in1=tt[:], op=ALU.add), reads=[btt, bxt], writes=[bxt])
                            S.dma("sp", XS[r0:r0 + 128, g * 512:(g + 1) * 512], xt_[:], reads=[bxt])
                S.barrier()

            if c.stop == 'P5':
                return True
            with ExitStack() as ph:
                T6 = min(1024, NT)
                NS = T6 // 512
                FC = c.FC
                hT6 = sb(ph, "h2T", [128, DC, T6], BF16)
                b_hT6 = Buf()
                wb6 = [sb(ph, "w6%d" % i, [128, DC, 512], BF16) for i in range(2)]
                bwb6 = [Buf(), Buf()]
                xb = sb(ph, "xb6", [128, D], F32)
                junk = sb(ph, "junk6", [128, D], BF16)
                ss = sb(ph, "ss6", [128, 4], F32)
                nb = (xb, Buf(), junk, Buf(), ss, Buf())
                fw = sb(ph, "fw", [128, 4, FC], F32)
                for j in range(3):
                    load_featmajor(ph, lambda c0, n, j=j: fw[:, j, c0:c0 + n], W["ffn_dw_w"][l][j], FC, 7, "fw%d" % j)
                load_featmajor(ph, lambda c0, n: fw[:, 3, c0:c0 + n], W["ffn_dw_b"][l], FC, 7, "fwb")
                halo = sb(ph, "halo", [128, FC, 2], F32)
                b_halo = Buf()
                S.op("dve", lambda e: e.memset(halo[:], 0.0), writes=[b_halo])
                sav = sb(ph, "sav", [128, FC, 4], F32)
                b_sav = Buf()
                stg = Ring([(sb(ph, "stg%d" % i, [128, 2 + 512], F32), Buf()) for i in range(3)])
                a6 = Ring([(sb(ph, "a6%d" % i, [128, 2, 512], F32), Buf()) for i in range(2)])
                ob6 = Ring([(sb(ph, "ob6%d" % i, [128, 512], BF16), Buf()) for i in range(3)])
                up = W["ffn_up"][l]
                ngr = FC // 2
                seq = [(t, g) for t in range(NT // T6) for g in range(ngr)]
                mmb = Ring([0, 1, 2, 3, 4, 5])

                def loadW(i):
                    t, g = seq[i]
                    S.dma("pool", wb6[i % 2][:, :, 0:256],
                          up[:, g * 256:(g + 1) * 256].rearrange("(kc p) n -> p kc n", p=128), writes=[bwb6[i % 2]])
                    S.dma("pool", wb6[i % 2][:, :, 256:512],
                          up[:, c.F + g * 256:c.F + (g + 1) * 256].rearrange("(kc p) n -> p kc n", p=128),
                          writes=[bwb6[i % 2]])
                loadW(0)
                for i, (t, g) in enumerate(seq):
                    t0 = t * T6
                    if g == 0:
                        for blk in range(T6 // 128):
                            norm_to_hT(nb, XS, t0 + blk * 128, blk, hT6, b_hT6, 1, 3, l, [6, 7])
                    if i + 1 < len(seq):
                        loadW(i + 1)
                    for j in range(2):
                        ci = g * 2 + j
                        for s in range(NS):
                            ts = t0 + s * 512
                            banks = []
                            for jj in (j, 2 + j):
                                bank = mmb.next()
                                banks.append(bank)

                                def mm(e, bank=bank, jj=jj, s=s, i=i):
                                    for kc in range(DC):
                                        ins = e.matmul(ps[bank][:], lhsT=wb6[i % 2][:, kc, jj * 128:(jj + 1) * 128],
                                                       rhs=hT6[:, kc, s * 512:(s + 1) * 512], start=(kc == 0),
                                                       stop=(kc == DC - 1))
                                    return ins
                                S.op("pe", mm, reads=[b_hT6, bwb6[i % 2]], writes=[bps[bank]])
                            sg_, bsg = stg.next()
                            S.op("act", lambda e, sg_=sg_, ci=ci: e.activation(out=sg_[:, 0:2], in_=halo[:, ci, :],
                                                                               func=AF.Copy),
                                 reads=[b_halo], writes=[bsg])
                            S.op("act", lambda e, sg_=sg_, ba=banks[0]: e.activation(out=sg_[:, 2:514], in_=ps[ba][:],
                                                                                     func=AF.Copy),
                                 reads=[bps[banks[0]]], writes=[bsg])
                            S.op("act", lambda e, sg_=sg_, ci=ci: e.activation(out=halo[:, ci, :], in_=sg_[:, 512:514],
                                                                               func=AF.Copy),
                                 reads=[bsg], writes=[b_halo])
                            if PAIR and t == 0 and s == 0:
                                S.op("act", lambda e, sg_=sg_, ci=ci: e.activation(
                                    out=sav[:, ci, 0:2], in_=sg_[:, 2:4], func=AF.Copy), reads=[bsg], writes=[b_sav])
                                S.op("dve", lambda e, ci=ci, bb_=banks[1]: e.tensor_copy(
                                    out=sav[:, ci, 2:4], in_=ps[bb_][:, 0:2]), reads=[bps[banks[1]]], writes=[b_sav])
                            at, bat = a6.next()
                            S.op("dve", lambda e, at=at, sg_=sg_, ci=ci: e.tensor_scalar(
                                out=at[:, 0, :], in0=sg_[:, 0:512], scalar1=fw[:, 0, ci:ci + 1],
                                scalar2=fw[:, 3, ci:ci + 1], op0=ALU.mult, op1=ALU.add),
                                reads=[bsg, bC], writes=[bat])
                            S.op("dve", lambda e, at=at, sg_=sg_, ci=ci: e.scalar_tensor_tensor(
                                out=at[:, 1, :], in0=sg_[:, 1:513], scalar=fw[:, 1, ci:ci + 1], in1=at[:, 0, :],
                                op0=ALU.mult, op1=ALU.add), reads=[bsg, bC, bat], writes=[bat])
                            S.op("dve", lambda e, at=at, sg_=sg_, ci=ci: e.scalar_tensor_tensor(
                                out=at[:, 0, :], in0=sg_[:, 2:514], scalar=fw[:, 2, ci:ci + 1], in1=at[:, 1, :],
                                op0=ALU.mult, op1=ALU.add), reads=[bsg, bC, bat], writes=[bat])
                            S.op("act", lambda e, at=at: e.activation(out=at[:, 1, :], in_=at[:, 0, :], func=AF.Silu),
                                 reads=[bat], writes=[bat])
                            ob, bob = ob6.next()
                            S.op("dve", lambda e, ob=ob, at=at, bb_=banks[1]: e.tensor_tensor(
                                out=ob[:], in0=at[:, 1, :], in1=ps[bb_][:], op=ALU.mult),
                                reads=[bat, bps[banks[1]]], writes=[bob])
                            S.dma("sp", ACTT[ci][:, ts:ts + 512], ob[:], reads=[bob])
                if PAIR:
                    S.dma("sp", HALOX.rearrange("c (p k) -> p c k", k=2), halo[:], reads=[b_halo])
                    S.barrier()
                    b_hg, b_rh, b_pw = Buf(), Buf(), Buf()
                    S.cc(HALOX, HG, RG, writes=[b_hg])
                    rh = sb(ph, "rh", [128, FC, 2], F32)
                    pw = sb(ph, "pw", [128, 6, FC], F32)
                    pa = sb(ph, "pa", [128, FC, 2], BF16)
                    S.dma("sp", rh[:], HG[0:FC, :].rearrange("c (p k) -> p c k", k=2), reads=[b_hg], writes=[b_rh])
                    S.op("dve", lambda e: e.tensor_scalar(out=rh[:], in0=rh[:], scalar1=flag[:, 0:1], scalar2=None,
                                                          op0=ALU.mult), reads=[b_rh, bC], writes=[b_rh])

                    def tt(o, a, b_, op):
                        S.op("dve", lambda e, o=o, a=a, b_=b_, op=op: e.tensor_tensor(out=o, in0=a, in1=b_, op=op),
                             reads=[b_rh, b_sav, bC], writes=[b_pw])
                    rh0, rh1 = rh[:, :, 0], rh[:, :, 1]
                    fa0, fa1, fb0, fb1 = sav[:, :, 0], sav[:, :, 1], sav[:, :, 2], sav[:, :, 3]
                    w0, w1, w2, wbb = fw[:, 0, :], fw[:, 1, :], fw[:, 2, :], fw[:, 3, :]
                    c0_, c1_, tmp_ = pw[:, 0, :], pw[:, 1, :], pw[:, 2, :]
                    for (cv, a0, a1, a2) in ((c0_, rh0, rh1, fa0), (c1_, rh1, fa0, fa1)):
                        tt(cv, w0, a0, ALU.mult)
                        tt(tmp_, w1, a1, ALU.mult)
                        tt(cv, cv, tmp_, ALU.add)
                        tt(tmp_, w2, a2, ALU.mult)
                        tt(cv, cv, tmp_, ALU.add)
                        tt(cv, cv, wbb, ALU.add)
                    S.op("act", lambda e: e.activation(out=pw[:, 3:5, :], in_=pw[:, 0:2, :], func=AF.Silu),
                         reads=[b_pw], writes=[b_pw])
                    tt(pa[:, :, 0], pw[:, 3, :], fb0, ALU.mult)
                    tt(pa[:, :, 1], pw[:, 4, :], fb1, ALU.mult)
                    S.dma("sp", ACTT[:, :, 0:2].rearrange("c p t -> p c t"), pa[:], reads=[b_pw])
                S.barrier()

            if c.stop == 'P6a':
                return True
            with ExitStack() as ph:
                FC = c.FC
                GW = 512
                NP = 4
                base, rem = FC // NP, FC % NP
                pieces = []
                k0 = 0
                for p_ in range(NP):
                    nk = base + (1 if p_ < rem else 0)
                    pieces.append((k0, nk))
                    k0 += nk
                KPM = max(nk for _, nk in pieces)
                NWB = 4
                aT = sb(ph, "aT", [128, FC, 512], BF16)
                b_aT = Buf()
                wb7 = [sb(ph, "w7%d" % i, [128, KPM, GW], BF16) for i in range(NWB)]
                bwb7 = [Buf() for _ in range(NWB)]
                g2b = sb(ph, "g2b", [128, D], F32)
                b_g2b = Buf()
                S.dma("sp", g2b[:], MODS[l, 5 * D:6 * D].partition_broadcast(128), writes=[b_g2b])
                tf7 = Ring([(sb(ph, "tf7%d" % i, [128, GW], F32), Buf()) for i in range(2)])
                xp = Ring([(sb(ph, "xq%d" % i, [128, GW], F32), Buf()) for i in range(3)])
                seq = [(t, g, p_) for t in range(NT // 512) for g in range(D // GW) for p_ in range(NP)]
                dn = W["ffn_down"][l]

                def loadW(i):
                    t, g, p_ = seq[i]
                    k0, nk = pieces[p_]
                    S.dma("pool", wb7[i % NWB][:, 0:nk, :],
                          dn[k0 * 128:(k0 + nk) * 128, g * GW:(g + 1) * GW].rearrange("(kc p) n -> p kc n", p=128),
                          writes=[bwb7[i % NWB]])
                for i in range(min(NWB - 1, len(seq))):
                    loadW(i)
                for i, (t, g, p_) in enumerate(seq):
                    t0 = t * 512
                    if g == 0 and p_ == 0:
                        S.dma("sp", aT[:], ACTT[:, :, t0:t0 + 512].rearrange("k p t -> p k t"), writes=[b_aT])
                    if i + NWB - 1 < len(seq):
                        loadW(i + NWB - 1)
                    k0, nk = pieces[p_]
                    bset = [0, 1, 2, 3] if g % 2 == 0 else [4, 5, 6, 7]
                    for blk in range(4):
                        r0 = t0 + blk * 128
                        bank = bset[blk]

                        def mm(e, bank=bank, blk=blk, i=i, k0=k0, nk=nk, p_=p_):
                            for kc in range(nk):
                                ins = e.matmul(ps[bank][:, 0:GW], lhsT=aT[:, k0 + kc, blk * 128:(blk + 1) * 128],
                                               rhs=wb7[i % NWB][:, kc, :], start=(p_ == 0 and kc == 0),
                                               stop=(p_ == NP - 1 and kc == nk - 1))
                            return ins
                        S.op("pe", mm, reads=[b_aT, bwb7[i % NWB]], writes=[bps[bank]])
                        if p_ == NP - 1:
                            xt_, bxt = xp.next()
                            S.dma("sp", xt_[:], XS[r0:r0 + 128, g * GW:(g + 1) * GW], writes=[bxt])
                            tt, btt = tf7.next()
                            S.op("dve", lambda e, tt=tt, bank=bank, g=g: e.tensor_tensor(
                                out=tt[:], in0=ps[bank][:, 0:GW], in1=g2b[:, g * GW:(g + 1) * GW], op=ALU.mult),
                                reads=[bps[bank], b_g2b], writes=[btt])
                            S.op("dve", lambda e, tt=tt, xt_=xt_: e.tensor_tensor(
                                out=xt_[:], in0=xt_[:], in1=tt[:], op=ALU.add), reads=[btt, bxt], writes=[bxt])
                            S.dma("sp", XS[r0:r0 + 128, g * GW:(g + 1) * GW], xt_[:], reads=[bxt])
                S.barrier()

        stopped = (c.stop == 'A0')
        for l in range(L):
            if not stopped:
                stopped = bool(layer(l))

        with ExitStack() as ph:
            fg = sb(ph, "fg", [128, D], F32)
            nblk_final = 0 if stopped else NT // 128
            b_fg = Buf()
            S.dma("sp", fg[:], W["final_g"].partition_broadcast(128), writes=[b_fg])
            xr = Ring([(sb(ph, "xf%d" % i, [128, D], F32), Buf()) for i in range(2)])
            junk = sb(ph, "junkf", [128, D], BF16)
            b_junk = Buf()
            ssr = Ring([(sb(ph, "ssf%d" % i, [128, 4], F32), Buf()) for i in range(2)])
            for blk in range(nblk_final):
                r0 = blk * 128
                xb, bxb = xr.next()
                ss, bss = ssr.next()
                S.dma("sp", xb[:], XS[r0:r0 + 128, :], writes=[bxb])
                S.op("act", lambda e, xb=xb, ss=ss: e.activation(out=junk[:], in_=xb[:], func=AF.Square,
                                                                 accum_out=ss[:, 0:1]),
                     reads=[bxb], writes=[b_junk, bss])
                S.op("act", lambda e, ss=ss: e.activation(out=ss[:, 1:2], in_=ss[:, 0:1], func=AF.Sqrt,
                                                          scale=1.0 / D, bias=1e-6), reads=[bss], writes=[bss])
                S.op("dve", lambda e, ss=ss: e.reciprocal(out=ss[:, 2:3], in_=ss[:, 1:2]), reads=[bss], writes=[bss])
                S.op("dve", lambda e, xb=xb, ss=ss: e.scalar_tensor_tensor(
                    out=xb[:], in0=xb[:], scalar=ss[:, 2:3], in1=fg[:], op0=ALU.mult, op1=ALU.mult),
                    reads=[bxb, bss, b_fg], writes=[bxb])
                S.dma("sp", y_out[r0:r0 + 128, :], xb[:], reads=[bxb])
            S.barrier()

        S.emit(block)
        print("[build] ops=%d waits=%d  pe=%d act=%d dve=%d pool=%d sp=%d" % (
            S.n_ops, S.n_waits, len(S.ops["pe"]), len(S.ops["act"]), len(S.ops["dve"]), len(S.ops["pool"]),
            len(S.ops["sp"])), flush=True)
    return nc


def run_cfg(c, inputs, n_cores=8):
    if c.pair:
        return run_pair(c, inputs, n_cores)
    nc = build_program(c)
    consts = make_consts(c)
    x = np.asarray(inputs["x"], dtype=np.float32)
    cvec = np.asarray(inputs["c"], dtype=np.float32)
    shared = {n: np.ascontiguousarray(np.asarray(inputs[n], dtype=np.float32)) for n in W_NAMES}
    in_maps = []
    for i in range(n_cores):
        b = i % c.B
        m = {"x": np.ascontiguousarray(x[b]), "c": np.ascontiguousarray(cvec[b:b + 1])}
        m.update(shared)
        m.update(consts)
        in_maps.append(m)
    res = run_bass_kernel_spmd(nc, in_maps, core_ids=list(range(n_cores)))
    out = np.stack([np.asarray(res.results[b]["y"]) for b in range(c.B)], axis=0)
    return out.astype(np.float32)


def run_pair(c, inputs, n_cores=8):
    assert n_cores == 2 * c.B
    nc = build_program(c)
    x = np.asarray(inputs["x"], dtype=np.float32)
    cvec = np.asarray(inputs["c"], dtype=np.float32)
    shared = {n: np.ascontiguousarray(np.asarray(inputs[n], dtype=np.float32)) for n in W_NAMES}
    consts = [make_consts(c, 0), make_consts(c, 1)]
    in_maps = []
    for i in range(n_cores):
        b, half = i // 2, i % 2
        m = {"x": np.ascontiguousarray(x[b, half * c.NT:(half + 1) * c.NT]), "c": np.ascontiguousarray(cvec[b:b + 1])}
        m.update(shared)
        m.update(consts[half])
        in_maps.append(m)
    res = run_bass_kernel_spmd(nc, in_maps, core_ids=list(range(n_cores)))
    out = np.empty((c.B, c.S, c.D), np.float32)
    for i in range(n_cores):
        b, half = i // 2, i % 2
        out[b, half * c.NT:(half + 1) * c.NT] = np.asarray(res.results[i]["y"])
    return out


def kernel(**inputs):
    c = Cfg(D=4096, S=4096, B=4, DEPTH=2, pair=True)
    return run_cfg(c, inputs)
```

```python
import math
from contextlib import ExitStack
import numpy as np
import concourse.bass as bass
import concourse.mybir as mybir
from concourse.bass_utils import run_bass_kernel_spmd

F32 = mybir.dt.float32
BF16 = mybir.dt.bfloat16
AF = mybir.ActivationFunctionType
ALU = mybir.AluOpType
AX = mybir.AxisListType


class Buf:
    __slots__ = ("name", "w", "r", "excl")

    def __init__(self, name="", excl=False):
        self.name = name
        self.w = None
        self.r = {}
        self.excl = excl


class Sched:
    ENG = ("pe", "act", "dve", "pool", "sp")

    def __init__(self, nc, n_dma_sems=40):
        self.nc = nc
        self.ops = {e: [] for e in self.ENG}
        self.count = {e: 0 for e in self.ENG}
        self.known = {e: {} for e in self.ENG}
        self.sems = {}
        self.n_dma_sems = n_dma_sems
        self.dma_cnt = [0] * n_dma_sems
        self.dma_rr = 0
        self.pool_rr = 0
        self.n_pool_sems = 8
        self.cc_cnt = 0
        self.n_waits = 0
        self.n_ops = 0

    def alloc_sems(self, stack):
        self.sems["cc"] = stack.enter_context(self.nc.semaphore("s_cc"))
        for e in self.ENG:
            self.sems[e] = stack.enter_context(self.nc.semaphore("s_" + e))
        for i in range(self.n_dma_sems):
            self.sems[("d", i)] = stack.enter_context(self.nc.semaphore("s_d%d" % i))

    def _need(self, eng, deps):
        kn = self.known[eng]
        best = {}
        for d in deps:
            if d is None:
                continue
            k, v = d
            if eng == "pe" and k == "pe":
                continue
            if kn.get(k, 0) >= v:
                continue
            if best.get(k, 0) < v:
                best[k] = v
        for k, v in best.items():
            kn[k] = v
            h = self.sems[k]
            self.ops[eng].append(lambda e, h=h, v=v: e.wait_ge(h, v))
            self.n_waits += 1

    @staticmethod
    def _deps(reads, writes):
        deps = []
        for b in reads:
            deps.append(b.w)
        for b in writes:
            deps.append(b.w)
            for kv in b.r.items():
                deps.append(kv)
        return deps

    @staticmethod
    def _commit(key, val, reads, writes):
        for b in reads:
            if b.r.get(key, 0) < val:
                b.r[key] = val
        for b in writes:
            b.w = (key, val)
            b.r = {}

    def op(self, eng, fn, reads=(), writes=()):
        ex = [b for b in reads if b.excl]
        if ex:
            reads = [b for b in reads if not b.excl]
            writes = list(writes) + ex
        self._need(eng, self._deps(reads, writes))
        self.count[eng] += 1
        val = self.count[eng]
        h = self.sems[eng]
        self.ops[eng].append(lambda e, fn=fn, h=h: fn(e).then_inc(h, 1))
        self._commit(eng, val, reads, writes)
        self.n_ops += 1

    def dma(self, q, out_ap, in_ap, reads=(), writes=(), **kw):
        if q == "pool":
            i = self.pool_rr
            self.pool_rr = (self.pool_rr + 1) % self.n_pool_sems
        else:
            i = self.n_pool_sems + self.dma_rr
            self.dma_rr = (self.dma_rr + 1) % (self.n_dma_sems - self.n_pool_sems)
        key = ("d", i)
        deps = self._deps(reads, writes)
        if self.dma_cnt[i] > 0:
            deps.append((key, self.dma_cnt[i]))
        self._need(q, deps)
        self.dma_cnt[i] += 16
        val = self.dma_cnt[i]
        h = self.sems[key]
        self.ops[q].append(
            lambda e, o=out_ap, a=in_ap, h=h, kw=kw: e.dma_start(out=o, in_=a, **kw).then_inc(h, 16))
        self._commit(key, val, reads, writes)
        self.n_ops += 1

    def cc(self, in_ap, out_ap, groups, reads=(), writes=()):
        key = "cc"
        self._need("pool", self._deps(reads, writes))
        self.cc_cnt += 1
        val = self.cc_cnt
        h = self.sems[key]
        self.ops["pool"].append(lambda e, i=in_ap, o=out_ap, h=h, g=groups: e.collective_compute(
            "AllGather", ALU.bypass, replica_groups=g, ins=[i], outs=[o]).then_inc(h, 1))
        self._commit(key, val, reads, writes)
        self.n_ops += 1

    def barrier(self, engines=None):
        allv = [(e, self.count[e]) for e in self.ENG if self.count[e] > 0]
        if self.cc_cnt:
            allv.append(("cc", self.cc_cnt))
        allv += [(("d", i), self.dma_cnt[i]) for i in range(self.n_dma_sems) if self.dma_cnt[i] > 0]
        for e in (engines or self.ENG):
            self._need(e, allv)

    def emit(self, block):
        ops = self.ops

        @block.tensor
        def _(e):
            for f in ops["pe"]:
                f(e)

        @block.scalar
        def _(e):
            for f in ops["act"]:
                f(e)

        @block.vector
        def _(e):
            for f in ops["dve"]:
                f(e)

        @block.gpsimd
        def _(e):
            for f in ops["pool"]:
                f(e)

        @block.sync
        def _(e):
            for f in ops["sp"]:
                f(e)


class Ring:
    def __init__(self, items):
        self.items = items
        self.i = 0

    def next(self):
        it = self.items[self.i]
        self.i = (self.i + 1) % len(self.items)
        return it


class StopBuild(Exception):
    pass


class Cfg:
    def __init__(self, D=4096, S=4096, B=4, DEPTH=2, stop=None, pair=False):
        self.stop = stop
        self.pair = pair
        self.ncores = 8
        self.ccrows = 512
        self.D, self.S, self.B, self.DEPTH = D, S, B, DEPTH
        self.DC = D // 128
        self.AW = D // 2
        self.H = self.AW // 256
        self.QK = self.H * 256
        self.SW = D // 4
        self.SG = self.SW // 128
        self.CW = D // 4
        self.CC = self.CW // 128
        self.CK = 31
        self.F = ((8 * D // 3 + 255) // 256) * 256
        self.FC = self.F // 128
        self.IN = 2 * self.QK + self.AW + 2 * self.SW + 2 * self.CW + 3 * D
        self.oq = 0
        self.ok = self.QK
        self.ov = 2 * self.QK
        self.ou = self.ov + self.AW
        self.osv = self.ou + self.SW
        self.oca = self.osv + self.SW
        self.ocb = self.oca + self.CW
        self.og = self.ocb + self.CW
        self.NT = S // 2 if pair else S


W_NAMES = ["ada_w", "ada_b", "norm1_g", "w_in", "b_gates", "lam_qk", "attn_norm_g", "sgu_ln_g", "sgu_ln_b",
           "sgu_w", "sgu_b", "conv_dw_w", "conv_dw_b", "conv_ln_g", "conv_ln_b", "w_proj_attn", "w_proj_sgu",
           "w_proj_conv", "w_out", "norm2_g", "ffn_up", "ffn_dw_w", "ffn_dw_b", "ffn_down", "final_g"]


def w_shapes(c):
    L, D = c.DEPTH, c.D
    return {
        "ada_w": [L, D, 6 * D], "ada_b": [L, 6 * D], "norm1_g": [L, D], "w_in": [L, D, c.IN],
        "b_gates": [L, 3 * D], "lam_qk": [L, 4, 128], "attn_norm_g": [L, 256], "sgu_ln_g": [L, c.SW],
        "sgu_ln_b": [L, c.SW], "sgu_w": [L, c.SG, 128, 128], "sgu_b": [L, c.SG, 128],
        "conv_dw_w": [L, c.CK, c.CW], "conv_dw_b": [L, c.CW], "conv_ln_g": [L, c.CW], "conv_ln_b": [L, c.CW],
        "w_proj_attn": [L, c.AW, D], "w_proj_sgu": [L, c.SW, D], "w_proj_conv": [L, c.CW, D],
        "w_out": [L, D, D], "norm2_g": [L, D], "ffn_up": [L, D, 2 * c.F], "ffn_dw_w": [L, 3, c.F],
        "ffn_dw_b": [L, c.F], "ffn_down": [L, c.F, D], "final_g": [D],
    }


def make_consts(c, hidx=0):
    half = 64
    inv = (10000.0 ** (-np.arange(half, dtype=np.float32) / np.float32(half))).astype(np.float32)
    pos = np.arange(c.S, dtype=np.float32)
    ang = (pos[:, None] * inv[None, :]).astype(np.float32)
    cos = np.cos(ang.astype(np.float64)).astype(np.float32).T
    sin = np.sin(ang.astype(np.float64)).astype(np.float32).T
    cos_tab = np.concatenate([cos, cos], 0)
    sin_tab = np.concatenate([-sin, sin], 0)
    ident = np.eye(128, dtype=np.float32)
    pswap = np.zeros((128, 128), np.float32)
    for m in range(128):
        pswap[(m + 64) % 128, m] = 1.0
    tri = np.triu(np.ones((128, 128), np.float32))
    p0 = hidx * c.NT
    return {"c_ident": ident, "c_pswap": pswap, "c_tri": tri,
            "c_flag": np.full((128, 1), float(hidx), np.float32),
            "c_cos": np.ascontiguousarray(cos_tab[:, p0:p0 + c.NT]),
            "c_sin": np.ascontiguousarray(sin_tab[:, p0:p0 + c.NT])}


def build_program(c):
    nc = bass.Bass("TRN2", target_bir_lowering=False)
    D, NT, DC = c.D, c.NT, c.DC
    L = c.DEPTH

    def din(name, shape):
        return nc.dram_tensor(name, list(shape), F32, kind="ExternalInput").ap()

    x_in = din("x", [NT, D])
    c_in = din("c", [1, D])
    W = {n: din(n, s) for n, s in w_shapes(c).items()}
    k_ident = din("c_ident", [128, 128])
    k_pswap = din("c_pswap", [128, 128])
    k_tri = din("c_tri", [128, 128])
    k_cos = din("c_cos", [128, NT])
    k_sin = din("c_sin", [128, NT])
    k_flag = din("c_flag", [128, 1])
    PAIR = c.pair
    RG = [[2 * i, 2 * i + 1] for i in range(c.ncores // 2)]
    y_out = nc.dram_tensor("y", [NT, D], F32, kind="ExternalOutput").ap()

    def dscr(name, shape, dt):
        return nc.dram_tensor(name, list(shape), dt, kind="Internal").ap()

    XS = dscr("xs", [NT, D], F32)
    MODS = dscr("mods", [L, 6 * D], F32)
    QT = dscr("qt", [c.QK // 128, 128, NT], BF16)
    KT = dscr("kt", [c.QK // 128, 128, NT], BF16)
    VV = dscr("vv", [NT, c.AW], BF16)
    UT = dscr("ut", [c.SG, 128, NT], BF16)
    SVR = dscr("svr", [NT, c.SW], F32)
    YT = dscr("yt", [c.CC, 128, 32 + NT], F32)
    GT = dscr("gt", [3 * DC, 128, NT], BF16)
    OAT = dscr("oat", [c.AW // 128, 128, NT], BF16)
    OST = dscr("ost", [c.SG, 128, NT], BF16)
    OCT = dscr("oct", [c.CC, 128, NT], BF16)
    ACTT = dscr("actt", [c.FC, 128, NT], BF16)
    HALO = 32
    NRC = (NT // 128) if PAIR else 0
    if PAIR:
        PR = c.ccrows
        KTG = dscr("ktg", [c.QK // PR, 2 * PR, NT], BF16)
        VVG = dscr("vvg", [NT // PR, 2 * PR, c.AW], BF16)
        YTAIL = dscr("ytail", [c.CC, 128 * 32], F32)
        YTG = dscr("ytg", [2 * c.CC, 128 * 32], F32)
        HALOX = dscr("halox", [c.FC, 128 * 2], F32)
        HG = dscr("hg", [2 * c.FC, 128 * 2], F32)

    with ExitStack() as st:
        S = Sched(nc)
        S.alloc_sems(st)

        uniq = [0]

        def sb(stack, name, shape, dt):
            uniq[0] += 1
            return stack.enter_context(nc.sbuf_tensor("%s_%d" % (name, uniq[0]), list(shape), dt))

        ident = sb(st, "ident", [128, 128], F32)
        ones_f = sb(st, "ones_f", [128, 128], F32)
        ones_b = sb(st, "ones_b", [128, 128], BF16)
        pswap = sb(st, "pswap", [128, 128], BF16)
        tri = sb(st, "tri", [128, 128], BF16)
        zero30 = sb(st, "zero30", [128, HALO], F32)
        flag = sb(st, "flag", [128, 1], F32)
        modT = sb(st, "modT", [128, L, 6 * DC], F32)
        gsc = sb(st, "gsc", [128, 2, DC], F32)
        stat1 = sb(st, "stat1", [128, NT // 128, 2], F32)
        stat2 = sb(st, "stat2", [128, NT // 128, 2], F32)
        bC = Buf("consts")
        b_modT, b_gsc, b_stat = Buf("modT"), Buf("gsc"), Buf("stat")
        ps = [st.enter_context(nc.psum_tensor("ps%d" % i, [128, 512], F32)) for i in range(8)]
        bps = [Buf("ps%d" % i, excl=True) for i in range(8)]
        block = st.enter_context(nc.Block())

        S.dma("sp", ident[:], k_ident, writes=[bC])
        S.dma("sp", flag[:], k_flag, writes=[bC])
        S.dma("pool", pswap[:], k_pswap, writes=[bC])
        S.dma("pool", tri[:], k_tri, writes=[bC])
        S.op("dve", lambda e: e.memset(ones_f[:], 1.0), writes=[bC])
        S.op("dve", lambda e: e.memset(ones_b[:], 1.0), writes=[bC])
        S.op("dve", lambda e: e.memset(zero30[:], 0.0), writes=[bC])
        for cc in range(c.CC):
            S.dma("sp", YT[cc][:, 0:HALO], zero30[:], reads=[bC])
        S.barrier()

        def load_featmajor(ph, dst_fn, vec_ap, n_ch, bank, tag):
            for c0 in range(0, n_ch, 128):
                n = min(128, n_ch - c0)
                tmp = sb(ph, "lf_%s_%d" % (tag, c0), [128, 128], F32)
                bt = Buf()
                S.dma("sp", tmp[0:n, :], vec_ap[c0 * 128:(c0 + n) * 128].rearrange("(c p) -> c p", p=128),
                      writes=[bt])
                S.op("pe", lambda e, tmp=tmp, n=n: e.transpose(out=ps[bank][:, 0:n], in_=tmp[0:n, :],
                                                                identity=ident[0:n, 0:n]),
                     reads=[bt, bC], writes=[bps[bank]])
                d = dst_fn(c0, n)
                S.op("dve", lambda e, d=d, n=n: e.tensor_copy(out=d, in_=ps[bank][:, 0:n]),
                     reads=[bps[bank]], writes=[bC])

        cTb = sb(st, "cTb", [128, DC], BF16)
        b_cT = Buf("cTb")

        def make_ada(l, ph, bank):
            wA = [sb(ph, "wA%d" % i, [128, DC, 512], BF16) for i in range(2)]
            bwA = [Buf(), Buf()]
            mrow = Ring([(sb(ph, "mrow%d" % i, [1, 512], F32), Buf()) for i in range(2)])
            abr = Ring([(sb(ph, "abr%d" % i, [1, 512], F32), Buf()) for i in range(2)])
            ng = 6 * D // 512
            state = {"i": 0}

            def load(i):
                S.dma("pool", wA[i % 2][:], W["ada_w"][l][:, i * 512:(i + 1) * 512].rearrange(
                    "(kc p) n -> p kc n", p=128), writes=[bwA[i % 2]])

            def step():
                i = state["i"]
                if i >= ng:
                    return False
                if i == 0:
                    load(0)
                if i + 1 < ng:
                    load(i + 1)
                ab, bab = abr.next()
                S.dma("sp", ab[:], W["ada_b"][l:l + 1, i * 512:(i + 1) * 512], writes=[bab])

                def mm(e, i=i):
                    for kc in range(DC):
                        ins = e.matmul(ps[bank][0:1, :], lhsT=cTb[:, kc:kc + 1], rhs=wA[i % 2][:, kc, :],
                                       start=(kc == 0), stop=(kc == DC - 1))
                    return ins
                S.op("pe", mm, reads=[b_cT, bwA[i % 2]], writes=[bps[bank]])
                mr, bmr = mrow.next()
                S.op("dve", lambda e, mr=mr, ab=ab: e.tensor_tensor(out=mr[:], in0=ps[bank][0:1, :], in1=ab[:],
                                                                    op=ALU.add),
                     reads=[bps[bank], bab], writes=[bmr])
                S.dma("sp", MODS[l:l + 1, i * 512:(i + 1) * 512], mr[:], reads=[bmr])
                state["i"] = i + 1
                return True
            return step

        def load_modT(l):
            with ExitStack() as ph2:
                load_featmajor(ph2, lambda c0, n, l=l: modT[:, l, c0:c0 + n], MODS[l], 6 * DC, 7, "m%d" % l)
                S.barrier()

        with ExitStack() as ph:
            cT = sb(ph, "cT", [128, DC], F32)
            load_featmajor(ph, lambda c0, n: cT[:, c0:c0 + n], c_in[0], DC, 7, "c")
            S.op("act", lambda e: e.activation(out=cTb[:], in_=cT[:], func=AF.Silu), reads=[bC], writes=[b_cT])
            step0 = make_ada(0, ph, 0)
            while step0():
                pass
            S.barrier()
        load_modT(0)

        def mod_vec(l, i):
            return modT[:, l, i * DC:(i + 1) * DC]

        def norm_to_hT(ph_bufs, xsrc, r0, nblk_col, hT, b_hT, sub, sh_i, l, banks):
            xb, b_xb, junk, b_junk, ss, b_ss = ph_bufs
            S.dma("sp", xb[:], xsrc[r0:r0 + 128, :], writes=[b_xb])
            S.op("act", lambda e: e.activation(out=junk[:], in_=xb[:], func=AF.Square, accum_out=ss[:, 0:1]),
                 reads=[b_xb], writes=[b_junk, b_ss])
            S.op("act", lambda e: e.activation(out=ss[:, 1:2], in_=ss[:, 0:1], func=AF.Sqrt, scale=1.0 / D,
                                               bias=1e-6), reads=[b_ss], writes=[b_ss])
            S.op("dve", lambda e: e.reciprocal(out=ss[:, 2:3], in_=ss[:, 1:2]), reads=[b_ss], writes=[b_ss])
            S.op("dve", lambda e: e.tensor_scalar(out=xb[:], in0=xb[:], scalar1=ss[:, 2:3], scalar2=None,
                                                  op0=ALU.mult), reads=[b_ss, b_xb], writes=[b_xb])
            for g in range(DC // 4):
                bank = banks[g % len(banks)]

                def tr(e, g=g, bank=bank):
                    for j in range(4):
                        kc = g * 4 + j
                        ins = e.transpose(out=ps[bank][:, j * 128:(j + 1) * 128],
                                          in_=xb[:, kc * 128:(kc + 1) * 128], identity=ident[:])
                    return ins
                S.op("pe", tr, reads=[b_xb, bC], writes=[bps[bank]])
                for j in range(4):
                    kc = g * 4 + j
                    dst = hT[:, kc, nblk_col * 128:(nblk_col + 1) * 128]
                    src = ps[bank][:, j * 128:(j + 1) * 128]
                    if g % 2 == 0:
                        S.op("act", lambda e, dst=dst, src=src, kc=kc: e.activation(
                            out=dst, in_=src, func=AF.Identity, scale=gsc[:, sub, kc:kc + 1],
                            bias=modT[:, l, sh_i * DC + kc:sh_i * DC + kc + 1]),
                            reads=[bps[bank], b_gsc, b_modT], writes=[b_hT])
                    else:
                        S.op("dve", lambda e, dst=dst, src=src, kc=kc: e.tensor_scalar(
                            out=dst, in0=src, scalar1=gsc[:, sub, kc:kc + 1],
                            scalar2=modT[:, l, sh_i * DC + kc:sh_i * DC + kc + 1], op0=ALU.mult, op1=ALU.add),
                            reads=[bps[bank], b_gsc, b_modT], writes=[b_hT])

        def layer(l):
            lam_init = 0.8 - 0.6 * math.exp(-0.3 * l)
            xcur = x_in if l == 0 else XS

            if l > 0:
                load_modT(l)
            with ExitStack() as ph:
                ng1 = sb(ph, "ng1", [128, DC], F32)
                load_featmajor(ph, lambda c0, n: ng1[:, c0:c0 + n], W["norm1_g"][l], DC, 7, "n1")
                ng2 = sb(ph, "ng2", [128, DC], F32)
                load_featmajor(ph, lambda c0, n: ng2[:, c0:c0 + n], W["norm2_g"][l], DC, 7, "n2")
                for sub, ng, sci in ((0, ng1, 1), (1, ng2, 4)):
                    S.op("dve", lambda e, sub=sub, ng=ng, sci=sci: e.scalar_tensor_tensor(
                        out=gsc[:, sub, :], in0=mod_vec(l, sci), scalar=1.0, in1=ng[:], op0=ALU.add,
                        op1=ALU.mult), reads=[bC, b_modT], writes=[b_gsc])
                S.barrier()

            with ExitStack() as ph:
                T1 = min(1024, NT)
                NS = T1 // 512
                hT1 = sb(ph, "hT1", [128, DC, T1], BF16)
                b_hT1 = Buf("hT1")
                wb1 = [sb(ph, "wb1%d" % i, [128, DC, 512], BF16) for i in range(2)]
                bwb1 = [Buf(), Buf()]
                xb = sb(ph, "xb", [128, D], F32)
                junk = sb(ph, "junk", [128, D], BF16)
                ss = sb(ph, "ss", [128, 4], F32)
                nb = (xb, Buf(), junk, Buf(), ss, Buf())
                cosT = sb(ph, "cosT", [128, T1], F32)
                sinT = sb(ph, "sinT", [128, T1], F32)
                b_cs = Buf()
                bgT = sb(ph, "bgT", [128, 3 * DC], F32)
                load_featmajor(ph, lambda c0, n: bgT[:, c0:c0 + n], W["b_gates"][l], 3 * DC, 7, "bg")
                stb = Ring([(sb(ph, "stb%d" % i, [128, 512], BF16), Buf()) for i in range(4)])
                stf = Ring([(sb(ph, "stf%d" % i, [128, 512], F32), Buf()) for i in range(4)])
                junk2 = sb(ph, "junk2", [128, 512], BF16)
                b_junk2 = Buf()
                mmb = Ring([0, 1, 2, 3])
                rtb = Ring([4, 5])

                groups = []
                win = W["w_in"][l]
                for c0 in range(0, 2 * c.QK, 512):
                    groups.append(("qk", [(c0, 512, 0)], c0))
                for c0 in range(0, c.AW, 512):
                    groups.append(("v", [(c.ov + c0, 512, 0)], c0))
                for c0 in range(0, c.SW, 512):
                    n = min(512, c.SW - c0)
                    groups.append(("u", [(c.ou + c0, n, 0)], c0))
                gw = min(512, c.SW)
                for c0 in range(0, c.SW, gw):
                    groups.append(("sv", [(c.osv + c0, gw, 0)], c0))
                for c0 in range(0, c.CW, 256):
                    groups.append(("conv", [(c.oca + c0, 256, 0), (c.ocb + c0, 256, 256)], c0))
                for c0 in range(0, 3 * D, 512):
                    groups.append(("gate", [(c.og + c0, 512, 0)], c0))
                import os as _os
                _kk = _os.environ.get('KKINDS')
                if _kk is not None:
                    groups = [g_ for g_ in groups if g_[0] in _kk.split(',')]
                    if not groups:
                        groups = [('none', [(0, 512, 0)], 0)]
                seq = [(t, gi) for t in range(NT // T1) for gi in range(len(groups))]

                def loadW(i):
                    t, gi = seq[i]
                    for (col, n, dcol) in groups[gi][1]:
                        S.dma("pool", wb1[i % 2][:, :, dcol:dcol + n],
                              win[:, col:col + n].rearrange("(kc p) n -> p kc n", p=128), writes=[bwb1[i % 2]])

                def fm_mm(i, j, s):
                    bank = mmb.next()

                    def mm(e, bank=bank):
                        for kc in range(DC):
                            ins = e.matmul(ps[bank][:, :], lhsT=wb1[i % 2][:, kc, j * 128:(j + 1) * 128],
                                           rhs=hT1[:, kc, s * 512:(s + 1) * 512], start=(kc == 0),
                                           stop=(kc == DC - 1))
                        return ins
                    S.op("pe", mm, reads=[b_hT1, bwb1[i % 2]], writes=[bps[bank]])
                    return bank

                def tm_mm(i, blk, n):
                    bank = mmb.next()

                    def mm(e, bank=bank):
                        for kc in range(DC):
                            ins = e.matmul(ps[bank][:, 0:n], lhsT=hT1[:, kc, blk * 128:(blk + 1) * 128],
                                           rhs=wb1[i % 2][:, kc, 0:n], start=(kc == 0), stop=(kc == DC - 1))
                        return ins
                    S.op("pe", mm, reads=[b_hT1, bwb1[i % 2]], writes=[bps[bank]])
                    return bank

                loadW(0)
                for i, (t, gi) in enumerate(seq):
                    t0 = t * T1
                    if gi == 0:
                        for blk in range(T1 // 128):
                            norm_to_hT(nb, xcur, t0 + blk * 128, blk, hT1, b_hT1, 0, 0, l, [6, 7])
                        S.dma("sp", cosT[:], k_cos[:, t0:t0 + T1], writes=[b_cs])
                        S.dma("sp", sinT[:], k_sin[:, t0:t0 + T1], writes=[b_cs])
                    if i + 1 < len(seq):
                        loadW(i + 1)
                    kind, pieces, c0 = groups[gi]
                    if kind == "qk":
                        for s in range(NS):
                            ts = t0 + s * 512
                            for j in range(4):
                                ci = c0 // 128 + j
                                dstT = QT[ci] if ci < c.QK // 128 else KT[ci - c.QK // 128]
                                bank = fm_mm(i, j, s)
                                zb, bzb = stb.next()
                                S.op("act", lambda e, zb=zb, bank=bank: e.activation(out=zb[:], in_=ps[bank][:],
                                                                                      func=AF.Copy),
                                     reads=[bps[bank]], writes=[bzb])
                                rb = rtb.next()
                                S.op("pe", lambda e, zb=zb, rb=rb: e.matmul(ps[rb][:], lhsT=pswap[:], rhs=zb[:],
                                                                            start=True, stop=True),
                                     reads=[bzb, bC], writes=[bps[rb]])
                                t1, bt1 = stf.next()
                                S.op("dve", lambda e, t1=t1, bank=bank, s=s: e.tensor_tensor(
                                    out=t1[:], in0=ps[bank][:], in1=cosT[:, s * 512:(s + 1) * 512], op=ALU.mult),
                                    reads=[bps[bank], b_cs], writes=[bt1])
                                t2, bt2 = stf.next()
                                S.op("dve", lambda e, t2=t2, rb=rb, s=s: e.tensor_tensor(
                                    out=t2[:], in0=ps[rb][:], in1=sinT[:, s * 512:(s + 1) * 512], op=ALU.mult),
                                    reads=[bps[rb], b_cs], writes=[bt2])
                                ob, bob = stb.next()
                                S.op("dve", lambda e, ob=ob, t1=t1, t2=t2: e.tensor_tensor(
                                    out=ob[:], in0=t1[:], in1=t2[:], op=ALU.add), reads=[bt1, bt2], writes=[bob])
                                S.dma("sp", dstT[:, ts:ts + 512], ob[:], reads=[bob])
                    elif kind == "u":
                        nj = pieces[0][1] // 128
                        for s in range(NS):
                            ts = t0 + s * 512
                            for j in range(nj):
                                ci = c0 // 128 + j
                                bank = fm_mm(i, j, s)
                                ob, bob = stb.next()
                                S.op("act", lambda e, ob=ob, bank=bank: e.activation(out=ob[:], in_=ps[bank][:],
                                                                                      func=AF.Gelu),
                                     reads=[bps[bank]], writes=[bob])
                                S.dma("sp", UT[ci][:, ts:ts + 512], ob[:], reads=[bob])
                    elif kind == "gate":
                        for s in range(NS):
                            ts = t0 + s * 512
                            for j in range(4):
                                ci = c0 // 128 + j
                                bank = fm_mm(i, j, s)
                                ob, bob = stb.next()
                                S.op("act", lambda e, ob=ob, bank=bank, ci=ci: e.activation(
                                    out=ob[:], in_=ps[bank][:], func=AF.Sigmoid, bias=bgT[:, ci:ci + 1]),
                                    reads=[bps[bank], bC], writes=[bob])
                                S.dma("sp", GT[ci][:, ts:ts + 512], ob[:], reads=[bob])
                    elif kind == "conv":
                        for s in range(NS):
                            ts = t0 + s * 512
                            for j in range(2):
                                ci = c0 // 128 + j
                                ba = fm_mm(i, j, s)
                                bb_ = fm_mm(i, 2 + j, s)
                                sg, bsg = stf.next()
                                S.op("act", lambda e, sg=sg, bb_=bb_: e.activation(out=sg[:], in_=ps[bb_][:],
                                                                                    func=AF.Sigmoid),
                                     reads=[bps[bb_]], writes=[bsg])
                                yo, byo = stf.next()
                                S.op("dve", lambda e, yo=yo, sg=sg, ba=ba: e.tensor_tensor(
                                    out=yo[:], in0=ps[ba][:], in1=sg[:], op=ALU.mult),
                                    reads=[bps[ba], bsg], writes=[byo])
                                S.dma("sp", YT[ci][:, HALO + ts:HALO + ts + 512], yo[:], reads=[byo])
                                if PAIR and ts + 512 == NT:
                                    S.dma("sp", YTAIL.rearrange("c (p k) -> c p k", k=32)[ci], yo[:, 480:512],
                                          reads=[byo])
                    elif kind == "v":
                        for blk in range(T1 // 128):
                            r0 = t0 + blk * 128
                            bank = tm_mm(i, blk, 512)
                            ob, bob = stb.next()
                            S.op("dve", lambda e, ob=ob, bank=bank: e.tensor_copy(out=ob[:], in_=ps[bank][:]),
                                 reads=[bps[bank]], writes=[bob])
                            S.dma("sp", VV[r0:r0 + 128, c0:c0 + 512], ob[:], reads=[bob])
                    elif kind == "sv":
                        g_i = c0 // gw
                        for blk in range(T1 // 128):
                            r0 = t0 + blk * 128
                            ba = r0 // 128
                            bank = tm_mm(i, blk, gw)
                            of, bof = stf.next()
                            S.op("act", lambda e, of=of, bank=bank, ba=ba, g_i=g_i: e.activation(
                                out=of[:, 0:gw], in_=ps[bank][:, 0:gw], func=AF.Gelu,
                                accum_out=stat1[:, ba, g_i:g_i + 1]), reads=[bps[bank]], writes=[bof, b_stat])
                            S.op("act", lambda e, of=of, ba=ba, g_i=g_i: e.activation(
                                out=junk2[:, 0:gw], in_=of[:, 0:gw], func=AF.Square,
                                accum_out=stat2[:, ba, g_i:g_i + 1]), reads=[bof], writes=[b_junk2, b_stat])
                            S.dma("sp", SVR[r0:r0 + 128, c0:c0 + gw], of[:, 0:gw], reads=[bof])
                S.barrier()

            b_ktg, b_vvg, b_ytg = Buf("ktg"), Buf("vvg"), Buf("ytg")
            if PAIR:
                with ExitStack() as ph:
                    kt2d = KT.rearrange("k p t -> (k p) t")
                    for p_ in range(c.QK // PR):
                        S.cc(kt2d[p_ * PR:(p_ + 1) * PR, :], KTG[p_], RG, writes=[b_ktg])
                    for p_ in range(NT // PR):
                        S.cc(VV[p_ * PR:(p_ + 1) * PR, :], VVG[p_], RG, writes=[b_vvg])
                    S.cc(YTAIL, YTG, RG, writes=[b_ytg])
                    yh = Ring([(sb(ph, "yh%d" % i, [128, 32], F32), Buf()) for i in range(2)])
                    for cc_ in range(c.CC):
                        t_, bt_ = yh.next()
                        S.dma("sp", t_[:], YTG.rearrange("c (p k) -> c p k", k=32)[cc_], reads=[b_ytg], writes=[bt_])
                        S.op("dve", lambda e, t_=t_: e.tensor_scalar(out=t_[:], in0=t_[:], scalar1=flag[:, 0:1],
                                                                     scalar2=None, op0=ALU.mult),
                             reads=[bt_, bC], writes=[bt_])
                        S.dma("sp", YT[cc_][:, 0:HALO], t_[:], reads=[bt_])
                    S.barrier()
            if c.stop == 'P1':
                return True
            with ExitStack() as ph:
                NG = NT // 512
                scale = 128.0 ** -0.5
                NK = NRC * 128 + NT
                NKS = NK // 512
                kt = [sb(ph, "kt%d" % i, [128, 2, NK], BF16) for i in range(2)]
                v1 = [sb(ph, "v1%d" % i, [128, NK // 128, 257], BF16) for i in range(2)]
                bkt = [Buf(), Buf()]
                bv1 = [Buf(), Buf()]
                qt = [sb(ph, "qtt%d" % i, [128, 2, 512], BF16) for i in range(2)]
                bqt = [Buf(), Buf()]
                sq = Ring([(sb(ph, "sq%d" % i, [128, 512], BF16), Buf()) for i in range(2)])
                pT = Ring([(sb(ph, "pT%d" % i, [128, 512], BF16), Buf()) for i in range(3)])
                kmx = sb(ph, "kmx", [128, 2, NKS + 2], F32)
                b_kmx = Buf()
                negm = [sb(ph, "negm%d" % i, [128, 512], BF16) for i in range(2)]
                bnegm = [Buf(), Buf()]
                on0 = sb(ph, "on0", [128, 4, 256], F32)
                b_on0 = Buf()
                od = Ring([(sb(ph, "od%d" % i, [128, 256], F32), Buf()) for i in range(2)])
                rl = sb(ph, "rl", [128, 8], F32)
                b_rl = Buf()
                junk3 = sb(ph, "junk3", [128, 256], BF16)
                b_junk3 = Buf()
                oT = Ring([(sb(ph, "oT%d" % i, [128, 2, 512], BF16), Buf()) for i in range(2)])
                gco = sb(ph, "gco", [128, 256], F32)
                lqb = sb(ph, "lqb", [128, 512], F32)
                lw = sb(ph, "lw", [128, 8], F32)
                b_lam = Buf()
                S.dma("sp", lqb[:], W["lam_qk"][l].rearrange("a b -> (a b)").partition_broadcast(128),
                      writes=[b_lam])
                S.dma("sp", gco[:], W["attn_norm_g"][l].partition_broadcast(128), writes=[b_lam])
                S.op("dve", lambda e: e.tensor_tensor(out=lqb[:, 0:128], in0=lqb[:, 0:128], in1=lqb[:, 128:256],
                                                      op=ALU.mult), reads=[b_lam], writes=[b_lam])
                S.op("dve", lambda e: e.tensor_tensor(out=lqb[:, 256:384], in0=lqb[:, 256:384],
                                                      in1=lqb[:, 384:512], op=ALU.mult),
                     reads=[b_lam], writes=[b_lam])
                S.op("dve", lambda e: e.reduce_sum(out=lw[:, 0:1], in_=lqb[:, 0:128], axis=AX.X),
                     reads=[b_lam], writes=[b_lam])
                S.op("dve", lambda e: e.reduce_sum(out=lw[:, 1:2], in_=lqb[:, 256:384], axis=AX.X),
                     reads=[b_lam], writes=[b_lam])
                S.op("act", lambda e: e.activation(out=lw[:, 2:4], in_=lw[:, 0:2], func=AF.Exp),
                     reads=[b_lam], writes=[b_lam])
                S.op("dve", lambda e: e.tensor_tensor(out=lw[:, 4:5], in0=lw[:, 3:4], in1=lw[:, 2:3],
                                                      op=ALU.subtract), reads=[b_lam], writes=[b_lam])
                S.op("dve", lambda e: e.tensor_scalar(out=lw[:, 4:5], in0=lw[:, 4:5], scalar1=-lam_init,
                                                      scalar2=None, op0=ALU.add), reads=[b_lam], writes=[b_lam])
                S.op("dve", lambda e: e.tensor_scalar(out=gco[:], in0=gco[:], scalar1=(1.0 - lam_init),
                                                      scalar2=None, op0=ALU.mult), reads=[b_lam], writes=[b_lam])
                for i in range(2):
                    S.op("dve", lambda e, i=i: e.memset(v1[i][:, :, 256:257], 1.0), writes=[bv1[i]])
                    if PAIR:
                        S.op("dve", lambda e, i=i: e.tensor_scalar(
                            out=v1[i][:, 0:NRC, 256:257], in0=v1[i][:, 0:NRC, 256:257], scalar1=flag[:, 0:1],
                            scalar2=None, op0=ALU.mult), reads=[bC], writes=[bv1[i]])
                SB = [0, 1]
                OB = [2, 3, 4, 5]
                MB = 6
                TB = 7
                ada_next = make_ada(l + 1, ph, MB) if l + 1 < L else (lambda: False)
                kmxs = [kmx, sb(ph, "kmx1", [128, 2, NKS + 2], F32)]
                b_kmxs = [b_kmx, Buf()]
                rl0 = sb(ph, "rl0", [128, 4], F32)
                rl1 = sb(ph, "rl1", [128, 4], F32)
                b_rl0, b_rl1 = Buf(), Buf()
                rts = Ring([(sb(ph, "rts%d" % i, [128, 4], F32), Buf()) for i in range(2)])
                ods = Ring([(sb(ph, "ods%d" % i, [128, 256], F32), Buf()) for i in range(4)])

                def head_setup(h):
                    kb, vb = kt[h % 2], v1[h % 2]
                    km, bkm = kmxs[h % 2], b_kmxs[h % 2]
                    for m in range(2):
                        S.dma("sp", kb[:, m, NRC * 128:NK], KT[2 * h + m], writes=[bkt[h % 2]])
                        if PAIR:
                            r_ = (2 * h + m) * 128
                            S.dma("sp", kb[:, m, 0:NRC * 128], KTG[r_ // PR][r_ % PR:r_ % PR + 128, :],
                                  reads=[b_ktg], writes=[bkt[h % 2]])
                    S.dma("sp", vb[:, NRC:NK // 128, 0:256],
                          VV[:, h * 256:(h + 1) * 256].rearrange("(j p) v -> p j v", p=128), writes=[bv1[h % 2]])
                    if PAIR:
                        for p_ in range(NT // PR):
                            S.dma("sp", vb[:, p_ * (PR // 128):(p_ + 1) * (PR // 128), 0:256],
                                  VVG[p_][0:PR, h * 256:(h + 1) * 256].rearrange("(j p) v -> p j v", p=128),
                                  reads=[b_vvg], writes=[bv1[h % 2]])
                        S.op("dve", lambda e, vb=vb: e.tensor_scalar(
                            out=vb[:, 0:NRC, 0:256], in0=vb[:, 0:NRC, 0:256], scalar1=flag[:, 0:1], scalar2=None,
                            op0=ALU.mult), reads=[bC], writes=[bv1[h % 2]])
                    for m in range(2):
                        for s in range(NKS):
                            sqt, bsq = sq.next()
                            S.op("dve", lambda e, sqt=sqt, m=m, s=s, kb=kb: e.tensor_tensor(
                                out=sqt[:], in0=kb[:, m, s * 512:(s + 1) * 512],
                                in1=kb[:, m, s * 512:(s + 1) * 512], op=ALU.mult),
                                reads=[bkt[h % 2]], writes=[bsq])
                            S.op("pe", lambda e, sqt=sqt: e.matmul(ps[MB][:, :], lhsT=ones_b[:, :], rhs=sqt[:],
                                                                    start=True, stop=True),
                                 reads=[bsq, bC], writes=[bps[MB]])
                            S.op("dve", lambda e, m=m, s=s, km=km: e.reduce_max(out=km[:, m, s:s + 1],
                                                                                 in_=ps[MB][:, :], axis=AX.X),
                                 reads=[bps[MB]], writes=[bkm])
                        S.op("dve", lambda e, m=m, km=km: e.reduce_max(out=km[:, m, NKS:NKS + 1], in_=km[:, m, 0:NKS],
                                                                       axis=AX.X), reads=[bkm], writes=[bkm])
                        S.op("dve", lambda e, m=m, km=km: e.tensor_scalar(
                            out=km[:, m, NKS + 1:NKS + 2], in0=km[:, m, NKS:NKS + 1], scalar1=-0.5 / 128.0,
                            scalar2=None, op0=ALU.mult), reads=[bkm], writes=[bkm])

                gstate = {}

                def group_setup(h, G):
                    hq = h * NG + G
                    qb, bq = qt[hq % 2], bqt[hq % 2]
                    for m in range(2):
                        S.dma("sp", qb[:, m, :], QT[2 * h + m][:, G * 512:G * 512 + 512], writes=[bq])
                    oTt, boT = oT.next()
                    gstate[(h, G)] = (qb, bq, oTt, boT)

                def prologue(h, G, m):
                    qb, bq, oTt, boT = gstate[(h, G)]
                    km, bkm = kmxs[h % 2], b_kmxs[h % 2]
                    sqt, bsq = sq.next()
                    S.op("dve", lambda e, sqt=sqt, m=m, qb=qb: e.tensor_tensor(
                        out=sqt[:], in0=qb[:, m, :], in1=qb[:, m, :], op=ALU.mult), reads=[bq], writes=[bsq])
                    S.op("pe", lambda e, sqt=sqt: e.matmul(ps[MB][:, :], lhsT=ones_b[:, :], rhs=sqt[:],
                                                            start=True, stop=True),
                         reads=[bsq, bC], writes=[bps[MB]])
                    nm = negm[m]
                    S.op("dve", lambda e, nm=nm, m=m, km=km: e.tensor_scalar(
                        out=nm[:], in0=ps[MB][:, :], scalar1=-0.5 / 128.0, scalar2=km[:, m, NKS + 1:NKS + 2],
                        op0=ALU.mult, op1=ALU.add), reads=[bps[MB], bkm], writes=[bnegm[m]])

                def mainloop(h, G, m):
                    qb, bq, oTt, boT = gstate[(h, G)]
                    kb, vb = kt[h % 2], v1[h % 2]
                    nm = negm[m]
                    jmax = NRC + 4 * (G + 1)

                    def qk(j):
                        jb = max(0, j - NRC - 4 * G)
                        bank = SB[j % 2]
                        cols = slice(jb * 128, 512)

                        def f(e):
                            e.matmul(ps[bank][:, cols], lhsT=kb[:, m, j * 128:(j + 1) * 128],
                                     rhs=qb[:, m, cols], start=True, stop=False)
                            return e.matmul(ps[bank][:, cols], lhsT=ones_b[:, :], rhs=nm[:, cols],
                                            start=False, stop=True)
                        S.op("pe", f, reads=[bkt[h % 2], bq, bnegm[m], bC], writes=[bps[bank]])
                    qk(0)
                    for j in range(jmax):
                        if j + 1 < jmax:
                            qk(j + 1)
                        jb = max(0, j - NRC - 4 * G)
                        bank = SB[j % 2]
                        cols = slice(jb * 128, 512)
                        pt, bpt = pT.next()
                        S.op("act", lambda e, pt=pt, bank=bank, cols=cols: e.activation(
                            out=pt[:, cols], in_=ps[bank][:, cols], func=AF.Exp, scale=scale),
                            reads=[bps[bank]], writes=[bpt])
                        if j >= NRC + 4 * G:
                            dc = slice(jb * 128, (jb + 1) * 128)
                            S.op("dve", lambda e, pt=pt, dc=dc: e.tensor_tensor(
                                out=pt[:, dc], in0=pt[:, dc], in1=tri[:], op=ALU.mult),
                                reads=[bpt, bC], writes=[bpt])

                        def av(e, pt=pt, j=j, jb=jb):
                            ins = None
                            for i in range(jb, 4):
                                ins = e.matmul(ps[OB[i]][:, 0:257], lhsT=pt[:, i * 128:(i + 1) * 128],
                                               rhs=vb[:, j, :], start=(j == 0), stop=(j == NRC + 4 * G + i))
                            return ins
                        S.op("pe", av, reads=[bpt, bv1[h % 2]], writes=[bps[OB[i]] for i in range(jb, 4)])

                def evac(h, G, m):
                    outs = []
                    for i in range(4):
                        if m == 0:
                            S.op("dve", lambda e, i=i: e.reciprocal(out=rl0[:, i:i + 1], in_=ps[OB[i]][:, 256:257]),
                                 reads=[bps[OB[i]]], writes=[b_rl0])
                            S.op("dve", lambda e, i=i: e.tensor_scalar(
                                out=on0[:, i, :], in0=ps[OB[i]][:, 0:256], scalar1=rl0[:, i:i + 1],
                                scalar2=None, op0=ALU.mult), reads=[bps[OB[i]], b_rl0], writes=[b_on0])
                        else:
                            S.op("dve", lambda e, i=i: e.reciprocal(out=rl1[:, i:i + 1], in_=ps[OB[i]][:, 256:257]),
                                 reads=[bps[OB[i]]], writes=[b_rl1])
                            S.op("dve", lambda e, i=i: e.tensor_scalar(
                                out=rl1[:, i:i + 1], in0=rl1[:, i:i + 1], scalar1=lw[:, 4:5],
                                scalar2=None, op0=ALU.mult), reads=[b_rl1, b_lam], writes=[b_rl1])
                            odt, bod = ods.next()
                            S.op("dve", lambda e, i=i, odt=odt: e.scalar_tensor_tensor(
                                out=odt[:], in0=ps[OB[i]][:, 0:256], scalar=rl1[:, i:i + 1],
                                in1=on0[:, i, :], op0=ALU.mult, op1=ALU.add),
                                reads=[bps[OB[i]], b_rl1, b_on0], writes=[bod])
                            outs.append((odt, bod))
                    return outs

                def finish(h, G, outs):
                    qb, bq, oTt, boT = gstate.pop((h, G))
                    for i, (odt, bod) in enumerate(outs):
                        rt, brt = rts.next()
                        S.op("act", lambda e, odt=odt, rt=rt: e.activation(
                            out=junk3[:], in_=odt[:], func=AF.Square, accum_out=rt[:, 0:1]),
                            reads=[bod], writes=[b_junk3, brt])
                        S.op("act", lambda e, rt=rt: e.activation(out=rt[:, 1:2], in_=rt[:, 0:1], func=AF.Sqrt,
                                                                  scale=1.0 / 256, bias=1e-6),
                             reads=[brt], writes=[brt])
                        S.op("dve", lambda e, rt=rt: e.reciprocal(out=rt[:, 2:3], in_=rt[:, 1:2]),
                             reads=[brt], writes=[brt])
                        S.op("dve", lambda e, odt=odt, rt=rt: e.scalar_tensor_tensor(
                            out=odt[:], in0=odt[:], scalar=rt[:, 2:3], in1=gco[:], op0=ALU.mult,
                            op1=ALU.mult), reads=[bod, brt, b_lam], writes=[bod])

                        def tr2(e, odt=odt):
                            e.transpose(out=ps[TB][:, 0:128], in_=odt[:, 0:128], identity=ident[:])
                            return e.transpose(out=ps[TB][:, 128:256], in_=odt[:, 128:256], identity=ident[:])
                        S.op("pe", tr2, reads=[bod, bC], writes=[bps[TB]])
                        S.op("act", lambda e, i=i, oTt=oTt: e.activation(
                            out=oTt[:, :, i * 128:(i + 1) * 128],
                            in_=ps[TB][:, 0:256].rearrange("p (a t) -> p a t", a=2), func=AF.Copy),
                            reads=[bps[TB]], writes=[boT])
                    for a in range(2):
                        S.dma("sp", OAT[2 * h + a][:, G * 512:G * 512 + 512], oTt[:, a, :], reads=[boT])

                combos = [(h, G, m) for h in range(c.H) for G in range(NG) for m in range(2)]

                def setup_for(idx):
                    h, G, m = combos[idx]
                    if m == 0:
                        if G == 0:
                            head_setup(h)
                        group_setup(h, G)
                    prologue(h, G, m)
                setup_for(0)
                for idx, (h, G, m) in enumerate(combos):
                    mainloop(h, G, m)
                    if idx + 1 < len(combos):
                        setup_for(idx + 1)
                    outs = evac(h, G, m)
                    if m == 1:
                        finish(h, G, outs)
                        ada_next()
                        ada_next()
                while ada_next():
                    pass
                S.barrier()

            if c.stop == 'P3':
                return True
            with ExitStack() as ph:
                SW, SG, CC, CK = c.SW, c.SG, c.CC, c.CK
                wsT = sb(ph, "wsT", [128, SG, 128], BF16)
                wtmp = sb(ph, "wtmp", [128, SG, 128], F32)
                sbb = sb(ph, "sbb", [128, SG, 4, 128], F32)
                lng = sb(ph, "lng", [128, SW], F32)
                lnb = sb(ph, "lnb", [128, SW], F32)
                bP = Buf("p4c")
                S.dma("sp", wtmp[:], W["sgu_w"][l].rearrange("g t s -> t g s"), writes=[bP])
                for a in range(4):
                    S.dma("sp", sbb[:, :, a, :], W["sgu_b"][l].partition_broadcast(128), writes=[bP])
                S.dma("sp", lng[:], W["sgu_ln_g"][l].partition_broadcast(128), writes=[bP])
                S.dma("sp", lnb[:], W["sgu_ln_b"][l].partition_broadcast(128), writes=[bP])
                for g in range(SG):
                    S.op("pe", lambda e, g=g: e.transpose(out=ps[7][:, 0:128], in_=wtmp[:, g, :], identity=ident[:]),
                         reads=[bP, bC], writes=[bps[7]])
                    S.op("dve", lambda e, g=g: e.tensor_tensor(out=wsT[:, g, :], in0=ps[7][:, 0:128], in1=tri[:],
                                                               op=ALU.mult), reads=[bps[7], bC], writes=[bP])
                cw = sb(ph, "cw", [128, CC, CK], F32)
                cwt = sb(ph, "cwt", [CK, c.CW], F32)
                S.dma("sp", cwt[:], W["conv_dw_w"][l], writes=[bP])
                for cc in range(CC):
                    S.op("pe", lambda e, cc=cc: e.transpose(out=ps[7][:, 0:CK], in_=cwt[0:CK, cc * 128:(cc + 1) * 128],
                                                            identity=ident[0:CK, 0:CK]),
                         reads=[bP, bC], writes=[bps[7]])
                    S.op("dve", lambda e, cc=cc: e.tensor_copy(out=cw[:, cc, :], in_=ps[7][:, 0:CK]),
                         reads=[bps[7]], writes=[bP])
                cvb = sb(ph, "cvb", [128, CC], F32)
                clg = sb(ph, "clg", [128, CC], F32)
                clb = sb(ph, "clb", [128, CC], F32)
                load_featmajor(ph, lambda c0, n: cvb[:, c0:c0 + n], W["conv_dw_b"][l], CC, 7, "cvb")
                load_featmajor(ph, lambda c0, n: clg[:, c0:c0 + n], W["conv_ln_g"][l], CC, 7, "clg")
                load_featmajor(ph, lambda c0, n: clb[:, c0:c0 + n], W["conv_ln_b"][l], CC, 7, "clb")

                svr = Ring([(sb(ph, "svr%d" % i, [128, SW], F32), Buf()) for i in range(2)])
                svn = Ring([(sb(ph, "svn%d" % i, [128, 4, SW], BF16), Buf()) for i in range(2)])
                utt = Ring([(sb(ph, "utt%d" % i, [128, SG, 512], BF16), Buf()) for i in range(2)])
                mst = sb(ph, "mst", [128, 8], F32)
                b_mst = Buf()
                tmpf = Ring([(sb(ph, "tmpf%d" % i, [128, 512], F32), Buf()) for i in range(3)])
                osb = Ring([(sb(ph, "osb%d" % i, [128, 512], BF16), Buf()) for i in range(3)])
                yin = Ring([(sb(ph, "yin%d" % i, [128, HALO + 512], F32), Buf()) for i in range(2)])
                acc = sb(ph, "acc", [128, CC, 2, 512], F32)
                b_acc = [Buf() for _ in range(CC)]
                sqa = Ring([(sb(ph, "sqa%d" % i, [128, 512], F32), Buf()) for i in range(2)])
                mrs = sb(ph, "mrs", [128, 4, 512], F32)
                b_mrs = Buf()
                for t in range(NT // 512):
                    t0 = t * 512
                    ut, but = utt.next()
                    S.dma("sp", ut[:], UT[:, :, t0:t0 + 512].rearrange("g p t -> p g t"), writes=[but])
                    sv, bsv = svn.next()
                    for n in range(4):
                        ba = t0 // 128 + n
                        sr, bsr = svr.next()
                        S.dma("sp", sr[:], SVR[t0 + n * 128:t0 + (n + 1) * 128, :], writes=[bsr])
                        ngr = (SW + 511) // 512
                        if ngr == 2:
                            S.op("dve", lambda e, ba=ba: e.tensor_tensor(out=mst[:, 0:1], in0=stat1[:, ba, 0:1],
                                                                         in1=stat1[:, ba, 1:2], op=ALU.add),
                                 reads=[b_stat], writes=[b_mst])
                            S.op("dve", lambda e, ba=ba: e.tensor_tensor(out=mst[:, 1:2], in0=stat2[:, ba, 0:1],
                                                                         in1=stat2[:, ba, 1:2], op=ALU.add),
                                 reads=[b_stat], writes=[b_mst])
                        else:
                            S.op("dve", lambda e, ba=ba: e.tensor_copy(out=mst[:, 0:1], in_=stat1[:, ba, 0:1]),
                                 reads=[b_stat], writes=[b_mst])
                            S.op("dve", lambda e, ba=ba: e.tensor_copy(out=mst[:, 1:2], in_=stat2[:, ba, 0:1]),
                                 reads=[b_stat], writes=[b_mst])
                        S.op("dve", lambda e: e.tensor_scalar(out=mst[:, 0:2], in0=mst[:, 0:2], scalar1=1.0 / SW,
                                                              scalar2=None, op0=ALU.mult),
                             reads=[b_mst], writes=[b_mst])
                        S.op("dve", lambda e: e.tensor_tensor(out=mst[:, 2:3], in0=mst[:, 0:1], in1=mst[:, 0:1],
                                                              op=ALU.mult), reads=[b_mst], writes=[b_mst])
                        S.op("dve", lambda e: e.tensor_tensor(out=mst[:, 3:4], in0=mst[:, 1:2], in1=mst[:, 2:3],
                                                              op=ALU.subtract), reads=[b_mst], writes=[b_mst])
                        S.op("act", lambda e: e.activation(out=mst[:, 4:5], in_=mst[:, 3:4], func=AF.Sqrt,
                                                           bias=1e-5), reads=[b_mst], writes=[b_mst])
                        S.op("dve", lambda e: e.reciprocal(out=mst[:, 5:6], in_=mst[:, 4:5]),
                             reads=[b_mst], writes=[b_mst])
                        S.op("dve", lambda e, sr=sr: e.tensor_scalar(out=sr[:], in0=sr[:], scalar1=mst[:, 0:1],
                                                                     scalar2=mst[:, 5:6], op0=ALU.subtract,
                                                                     op1=ALU.mult),
                             reads=[bsr, b_mst], writes=[bsr])
                        S.op("dve", lambda e, sr=sr: e.tensor_tensor(out=sr[:], in0=sr[:], in1=lng[:], op=ALU.mult),
                             reads=[bsr, bP], writes=[bsr])
                        S.op("dve", lambda e, sr=sr, sv=sv, n=n: e.tensor_tensor(out=sv[:, n, :], in0=sr[:],
                                                                                 in1=lnb[:], op=ALU.add),
                             reads=[bsr, bP], writes=[bsv])
                    for g in range(SG):
                        bank = g % 2

                        def mm(e, g=g, bank=bank, sv=sv):
                            for n in range(4):
                                ins = e.matmul(ps[bank][:, n * 128:(n + 1) * 128],
                                               lhsT=sv[:, n, g * 128:(g + 1) * 128], rhs=wsT[:, g, :],
                                               start=True, stop=True)
                            return ins
                        S.op("pe", mm, reads=[bsv, bP], writes=[bps[bank]])
                        tf, btf = tmpf.next()
                        S.op("dve", lambda e, tf=tf, bank=bank, g=g: e.tensor_tensor(
                            out=tf[:], in0=ps[bank][:], in1=sbb[:, g, :, :].rearrange("p a t -> p (a t)"),
                            op=ALU.add), reads=[bps[bank], bP], writes=[btf])
                        ob, bob = osb.next()
                        S.op("dve", lambda e, ob=ob, tf=tf, ut=ut, g=g: e.tensor_tensor(
                            out=ob[:], in0=tf[:], in1=ut[:, g, :], op=ALU.mult), reads=[btf, but], writes=[bob])
                        S.dma("sp", OST[g][:, t0:t0 + 512], ob[:], reads=[bob])
                    for cc in range(CC):
                        yi, byi = yin.next()
                        S.dma("sp", yi[:], YT[cc][:, t0:t0 + HALO + 512], writes=[byi])
                        off = HALO - (CK - 1)
                        S.op("dve", lambda e, yi=yi, cc=cc, off=off: e.tensor_scalar(
                            out=acc[:, cc, 0, :], in0=yi[:, off:off + 512], scalar1=cw[:, cc, 0:1],
                            scalar2=cvb[:, cc:cc + 1], op0=ALU.mult, op1=ALU.add),
                            reads=[byi, bP], writes=[b_acc[cc]])
                        for j in range(1, CK):
                            S.op("dve", lambda e, yi=yi, cc=cc, off=off, j=j: e.scalar_tensor_tensor(
                                out=acc[:, cc, j % 2, :], in0=yi[:, off + j:off + j + 512],
                                scalar=cw[:, cc, j:j + 1], in1=acc[:, cc, (j - 1) % 2, :], op0=ALU.mult,
                                op1=ALU.add), reads=[byi, bP, b_acc[cc]], writes=[b_acc[cc]])
                        fin = (CK - 1) % 2
                        sqt, bsq = sqa.next()
                        S.op("act", lambda e, sqt=sqt, cc=cc, fin=fin: e.activation(
                            out=sqt[:], in_=acc[:, cc, fin, :], func=AF.Square), reads=[b_acc[cc]], writes=[bsq])
                        S.op("pe", lambda e, cc=cc, fin=fin: e.matmul(ps[2][:], lhsT=ones_f[:], rhs=acc[:, cc, fin, :],
                                                                      start=(cc == 0), stop=(cc == CC - 1)),
                             reads=[b_acc[cc], bC], writes=[bps[2]])
                        S.op("pe", lambda e, cc=cc, sqt=sqt: e.matmul(ps[3][:], lhsT=ones_f[:], rhs=sqt[:],
                                                                      start=(cc == 0), stop=(cc == CC - 1)),
                             reads=[bsq, bC], writes=[bps[3]])
                    fin = (CK - 1) % 2
                    S.op("dve", lambda e: e.tensor_scalar(out=mrs[:, 0, :], in0=ps[2][:], scalar1=1.0 / c.CW,
                                                          scalar2=None, op0=ALU.mult),
                         reads=[bps[2]], writes=[b_mrs])
                    S.op("dve", lambda e: e.tensor_scalar(out=mrs[:, 1, :], in0=ps[3][:], scalar1=1.0 / c.CW,
                                                          scalar2=None, op0=ALU.mult),
                         reads=[bps[3]], writes=[b_mrs])
                    S.op("dve", lambda e: e.tensor_tensor(out=mrs[:, 2, :], in0=mrs[:, 0, :], in1=mrs[:, 0, :],
                                                          op=ALU.mult), reads=[b_mrs], writes=[b_mrs])
                    S.op("dve", lambda e: e.tensor_tensor(out=mrs[:, 1, :], in0=mrs[:, 1, :], in1=mrs[:, 2, :],
                                                          op=ALU.subtract), reads=[b_mrs], writes=[b_mrs])
                    S.op("act", lambda e: e.activation(out=mrs[:, 2, :], in_=mrs[:, 1, :], func=AF.Sqrt, bias=1e-5),
                         reads=[b_mrs], writes=[b_mrs])
                    S.op("dve", lambda e: e.reciprocal(out=mrs[:, 3, :], in_=mrs[:, 2, :]),
                         reads=[b_mrs], writes=[b_mrs])
                    for cc in range(CC):
                        tf, btf = tmpf.next()
                        S.op("dve", lambda e, tf=tf, cc=cc, fin=fin: e.tensor_tensor(
                            out=tf[:], in0=acc[:, cc, fin, :], in1=mrs[:, 0, :], op=ALU.subtract),
                            reads=[b_acc[cc], b_mrs], writes=[btf])
                        S.op("dve", lambda e, tf=tf: e.tensor_tensor(out=tf[:], in0=tf[:], in1=mrs[:, 3, :],
                                                                     op=ALU.mult), reads=[btf, b_mrs], writes=[btf])
                        ob, bob = osb.next()
                        S.op("act", lambda e, ob=ob, tf=tf, cc=cc: e.activation(
                            out=ob[:], in_=tf[:], func=AF.Silu, scale=clg[:, cc:cc + 1], bias=clb[:, cc:cc + 1]),
                            reads=[btf, bC], writes=[bob])
                        S.dma("sp", OCT[cc][:, t0:t0 + 512], ob[:], reads=[bob])
                S.barrier()

            if c.stop == 'P4':
                return True
            with ExitStack() as ph:
                AC, SG, CC = c.AW // 128, c.SG, c.CC
                KC5 = AC + SG + CC
                assert KC5 == DC
                bT = sb(ph, "bT", [128, DC, 512], BF16)
                b_bT = Buf()
                mT = sb(ph, "mT", [128, DC, 512], BF16)
                b_mT = Buf()
                NW5 = 3
                wb5 = [sb(ph, "w5%d" % i, [128, DC, 512], BF16) for i in range(NW5)]
                bwb5 = [[Buf(), Buf(), Buf()] for _ in range(NW5)]
                g1r = Ring([(sb(ph, "g1r%d" % i, [128, 512], F32), Buf()) for i in range(2)])
                gts = Ring([(sb(ph, "gts%d" % i, [128, 3, 512], BF16), Buf()) for i in range(4)])
                mf = Ring([(sb(ph, "mf%d" % i, [128, 512], F32), Buf()) for i in range(3)])
                tf5 = Ring([(sb(ph, "tf5%d" % i, [128, 512], F32), Buf()) for i in range(4)])
                xp = Ring([(sb(ph, "xp%d" % i, [128, 512], F32), Buf()) for i in range(4)])
                NTL = NT // 512
                seq = [(t, ph_, g) for t in range(NTL) for ph_ in (0, 1) for g in range(D // 512)]

                def loadW(i):
                    t, ph_, g = seq[i]
                    cs = slice(g * 512, (g + 1) * 512)
                    buf = wb5[i % NW5]
                    if ph_ == 0:
                        for pi_, (src, k0, nk) in enumerate(((W["w_proj_attn"][l], 0, AC), (W["w_proj_sgu"][l], AC, SG),
                                                             (W["w_proj_conv"][l], AC + SG, CC))):
                            S.dma("pool", buf[:, k0:k0 + nk, :], src[:, cs].rearrange("(kc p) n -> p kc n", p=128),
                                  writes=[bwb5[i % NW5][pi_]])
                    else:
                        S.dma("pool", buf[:], W["w_out"][l][:, cs].rearrange("(kc p) n -> p kc n", p=128),
                              writes=bwb5[i % NW5])
                for i_ in range(min(NW5 - 1, len(seq))):
                    loadW(i_)
                mm5 = Ring([0, 1, 2, 3, 4, 5])
                for i, (t, ph_, g) in enumerate(seq):
                    t0 = t * 512
                    if ph_ == 0 and g == 0:
                        S.dma("sp", bT[:, 0:AC, :], OAT[:, :, t0:t0 + 512].rearrange("k p t -> p k t"), writes=[b_bT])
                        S.dma("sp", bT[:, AC:AC + SG, :], OST[:, :, t0:t0 + 512].rearrange("k p t -> p k t"),
                              writes=[b_bT])
                        S.dma("sp", bT[:, AC + SG:DC, :], OCT[:, :, t0:t0 + 512].rearrange("k p t -> p k t"),
                              writes=[b_bT])
                    if i + NW5 - 1 < len(seq):
                        loadW(i + NW5 - 1)
                    if ph_ == 0:
                        for j in range(4):
                            ci = g * 4 + j
                            gt, bgt = gts.next()
                            S.dma("sp", gt[:], GT[:, :, t0:t0 + 512].rearrange("(b c) p t -> c p b t", b=3)[ci],
                                  writes=[bgt])
                            banks = []
                            for (k0, nk) in ((0, AC), (AC, SG), (AC + SG, CC)):
                                bank = mm5.next()
                                banks.append(bank)

                                def mm(e, bank=bank, k0=k0, nk=nk, j=j, i=i):
                                    for kc in range(k0, k0 + nk):
                                        ins = e.matmul(ps[bank][:], lhsT=wb5[i % NW5][:, kc, j * 128:(j + 1) * 128],
                                                       rhs=bT[:, kc, :], start=(kc == k0), stop=(kc == k0 + nk - 1))
                                    return ins
                                S.op("pe", mm, reads=[b_bT] + bwb5[i % NW5], writes=[bps[bank]])
                            m_, bm_ = mf.next()
                            S.op("dve", lambda e, m_=m_, gt=gt, b0=banks[0]: e.tensor_tensor(
                                out=m_[:], in0=ps[b0][:], in1=gt[:, 0, :], op=ALU.mult),
                                reads=[bps[banks[0]], bgt], writes=[bm_])
                            for bi in (1, 2):
                                tt, btt = tf5.next()
                                S.op("dve", lambda e, tt=tt, gt=gt, bk=banks[bi], bi=bi: e.tensor_tensor(
                                    out=tt[:], in0=ps[bk][:], in1=gt[:, bi, :], op=ALU.mult),
                                    reads=[bps[banks[bi]], bgt], writes=[btt])
                                if bi == 1:
                                    S.op("dve", lambda e, m_=m_, tt=tt: e.tensor_tensor(
                                        out=m_[:], in0=m_[:], in1=tt[:], op=ALU.add), reads=[bm_, btt], writes=[bm_])
                                else:
                                    S.op("dve", lambda e, m_=m_, tt=tt, ci=ci: e.tensor_tensor(
                                        out=mT[:, ci, :], in0=m_[:], in1=tt[:], op=ALU.add),
                                        reads=[bm_, btt], writes=[b_mT])
                    else:
                        g1b, b_g1b = g1r.next()
                        S.dma("sp", g1b[:], MODS[l, 2 * D + g * 512:2 * D + (g + 1) * 512].partition_broadcast(128),
                              writes=[b_g1b])
                        for blk in range(4):
                            r0 = t0 + blk * 128
                            bank = mm5.next()

                            def mm(e, bank=bank, blk=blk, i=i):
                                for kc in range(DC):
                                    ins = e.matmul(ps[bank][:], lhsT=mT[:, kc, blk * 128:(blk + 1) * 128],
                                                   rhs=wb5[i % NW5][:, kc, :], start=(kc == 0), stop=(kc == DC - 1))
                                return ins
                            S.op("pe", mm, reads=[b_mT] + bwb5[i % NW5], writes=[bps[bank]])
                            xt_, bxt = xp.next()
                            S.dma("sp", xt_[:], xcur[r0:r0 + 128, g * 512:(g + 1) * 512], writes=[bxt])
                            tt, btt = tf5.next()
                            S.op("dve", lambda e, tt=tt, bank=bank, g1b=g1b: e.tensor_tensor(
                                out=tt[:], in0=ps[bank][:], in1=g1b[:], op=ALU.mult),
                                reads=[bps[bank], b_g1b], writes=[btt])
                            S.op("dve", lambda e, tt=tt, xt_=xt_: e.tensor_tensor(
                                out=xt_[:], in0=xt_[:], in1=tt[:], op=ALU.add), reads=[btt, bxt], writes=[bxt])
                            S.dma("sp", XS[r0:r0 + 128, g * 512:(g + 1) * 512], xt_[:], reads=[bxt])
                S.barrier()

            if c.stop == 'P5':
                return True
            with ExitStack() as ph:
                T6 = min(1024, NT)
                NS = T6 // 512
                FC = c.FC
                hT6 = sb(ph, "h2T", [128, DC, T6], BF16)
                b_hT6 = Buf()
                wb6 = [sb(ph, "w6%d" % i, [128, DC, 512], BF16) for i in range(2)]
                bwb6 = [[Buf(), Buf()], [Buf(), Buf()]]
                xb = sb(ph, "xb6", [128, D], F32)
                junk = sb(ph, "junk6", [128, D], BF16)
                ss = sb(ph, "ss6", [128, 4], F32)
                nb = (xb, Buf(), junk, Buf(), ss, Buf())
                fw = sb(ph, "fw", [128, 4, FC], F32)
                for j in range(3):
                    load_featmajor(ph, lambda c0, n, j=j: fw[:, j, c0:c0 + n], W["ffn_dw_w"][l][j], FC, 7, "fw%d" % j)
                load_featmajor(ph, lambda c0, n: fw[:, 3, c0:c0 + n], W["ffn_dw_b"][l], FC, 7, "fwb")
                halo = sb(ph, "halo", [128, FC, 2], F32)
                b_halo = Buf()
                S.op("dve", lambda e: e.memset(halo[:], 0.0), writes=[b_halo])
                sav = sb(ph, "sav", [128, FC, 4], F32)
                b_sav = Buf()
                stg = Ring([(sb(ph, "stg%d" % i, [128, 2 + 512], F32), Buf()) for i in range(3)])
                a6 = Ring([(sb(ph, "a6%d" % i, [128, 2, 512], F32), Buf()) for i in range(2)])
                ob6 = Ring([(sb(ph, "ob6%d" % i, [128, 512], BF16), Buf()) for i in range(3)])
                up = W["ffn_up"][l]
                ngr = FC // 2
                seq = [(t, g) for t in range(NT // T6) for g in range(ngr)]
                mmb = Ring([0, 1, 2, 3, 4, 5])

                def loadW(i):
                    t, g = seq[i]
                    S.dma("pool", wb6[i % 2][:, :, 0:256],
                          up[:, g * 256:(g + 1) * 256].rearrange("(kc p) n -> p kc n", p=128), writes=[bwb6[i % 2][0]])
                    S.dma("pool", wb6[i % 2][:, :, 256:512],
                          up[:, c.F + g * 256:c.F + (g + 1) * 256].rearrange("(kc p) n -> p kc n", p=128),
                          writes=[bwb6[i % 2][1]])
                loadW(0)
                for i, (t, g) in enumerate(seq):
                    t0 = t * T6
                    if g == 0:
                        for blk in range(T6 // 128):
                            norm_to_hT(nb, XS, t0 + blk * 128, blk, hT6, b_hT6, 1, 3, l, [6, 7])
                    if i + 1 < len(seq):
                        loadW(i + 1)
                    for j in range(2):
                        ci = g * 2 + j
                        for s in range(NS):
                            ts = t0 + s * 512
                            banks = []
                            for jj in (j, 2 + j):
                                bank = mmb.next()
                                banks.append(bank)

                                def mm(e, bank=bank, jj=jj, s=s, i=i):
                                    for kc in range(DC):
                                        ins = e.matmul(ps[bank][:], lhsT=wb6[i % 2][:, kc, jj * 128:(jj + 1) * 128],
                                                       rhs=hT6[:, kc, s * 512:(s + 1) * 512], start=(kc == 0),
                                                       stop=(kc == DC - 1))
                                    return ins
                                S.op("pe", mm, reads=[b_hT6] + bwb6[i % 2], writes=[bps[bank]])
                            sg_, bsg = stg.next()
                            S.op("act", lambda e, sg_=sg_, ci=ci: e.activation(out=sg_[:, 0:2], in_=halo[:, ci, :],
                                                                               func=AF.Copy),
                                 reads=[b_halo], writes=[bsg])
                            S.op("act", lambda e, sg_=sg_, ba=banks[0]: e.activation(out=sg_[:, 2:514], in_=ps[ba][:],
                                                                                     func=AF.Copy),
                                 reads=[bps[banks[0]]], writes=[bsg])
                            S.op("act", lambda e, sg_=sg_, ci=ci: e.activation(out=halo[:, ci, :], in_=sg_[:, 512:514],
                                                                               func=AF.Copy),
                                 reads=[bsg], writes=[b_halo])
                            if PAIR and t == 0 and s == 0:
                                S.op("act", lambda e, sg_=sg_, ci=ci: e.activation(
                                    out=sav[:, ci, 0:2], in_=sg_[:, 2:4], func=AF.Copy), reads=[bsg], writes=[b_sav])
                                S.op("dve", lambda e, ci=ci, bb_=banks[1]: e.tensor_copy(
                                    out=sav[:, ci, 2:4], in_=ps[bb_][:, 0:2]), reads=[bps[banks[1]]], writes=[b_sav])
                            at, bat = a6.next()
                            S.op("dve", lambda e, at=at, sg_=sg_, ci=ci: e.tensor_scalar(
                                out=at[:, 0, :], in0=sg_[:, 0:512], scalar1=fw[:, 0, ci:ci + 1],
                                scalar2=fw[:, 3, ci:ci + 1], op0=ALU.mult, op1=ALU.add),
                                reads=[bsg, bC], writes=[bat])
                            S.op("dve", lambda e, at=at, sg_=sg_, ci=ci: e.scalar_tensor_tensor(
                                out=at[:, 1, :], in0=sg_[:, 1:513], scalar=fw[:, 1, ci:ci + 1], in1=at[:, 0, :],
                                op0=ALU.mult, op1=ALU.add), reads=[bsg, bC, bat], writes=[bat])
                            S.op("dve", lambda e, at=at, sg_=sg_, ci=ci: e.scalar_tensor_tensor(
                                out=at[:, 0, :], in0=sg_[:, 2:514], scalar=fw[:, 2, ci:ci + 1], in1=at[:, 1, :],
                                op0=ALU.mult, op1=ALU.add), reads=[bsg, bC, bat], writes=[bat])
                            S.op("act", lambda e, at=at: e.activation(out=at[:, 1, :], in_=at[:, 0, :], func=AF.Silu),
                                 reads=[bat], writes=[bat])
                            ob, bob = ob6.next()
                            S.op("dve", lambda e, ob=ob, at=at, bb_=banks[1]: e.tensor_tensor(
                                out=ob[:], in0=at[:, 1, :], in1=ps[bb_][:], op=ALU.mult),
                                reads=[bat, bps[banks[1]]], writes=[bob])
                            S.dma("sp", ACTT[ci][:, ts:ts + 512], ob[:], reads=[bob])
                if PAIR:
                    S.dma("sp", HALOX.rearrange("c (p k) -> p c k", k=2), halo[:], reads=[b_halo])
                    S.barrier()
                    b_hg, b_rh, b_pw = Buf(), Buf(), Buf()
                    S.cc(HALOX, HG, RG, writes=[b_hg])
                    rh = sb(ph, "rh", [128, FC, 2], F32)
                    pw = sb(ph, "pw", [128, 6, FC], F32)
                    pa = sb(ph, "pa", [128, FC, 2], BF16)
                    S.dma("sp", rh[:], HG[0:FC, :].rearrange("c (p k) -> p c k", k=2), reads=[b_hg], writes=[b_rh])
                    S.op("dve", lambda e: e.tensor_scalar(out=rh[:], in0=rh[:], scalar1=flag[:, 0:1], scalar2=None,
                                                          op0=ALU.mult), reads=[b_rh, bC], writes=[b_rh])

                    def tt(o, a, b_, op):
                        S.op("dve", lambda e, o=o, a=a, b_=b_, op=op: e.tensor_tensor(out=o, in0=a, in1=b_, op=op),
                             reads=[b_rh, b_sav, bC], writes=[b_pw])
                    rh0, rh1 = rh[:, :, 0], rh[:, :, 1]
                    fa0, fa1, fb0, fb1 = sav[:, :, 0], sav[:, :, 1], sav[:, :, 2], sav[:, :, 3]
                    w0, w1, w2, wbb = fw[:, 0, :], fw[:, 1, :], fw[:, 2, :], fw[:, 3, :]
                    c0_, c1_, tmp_ = pw[:, 0, :], pw[:, 1, :], pw[:, 2, :]
                    for (cv, a0, a1, a2) in ((c0_, rh0, rh1, fa0), (c1_, rh1, fa0, fa1)):
                        tt(cv, w0, a0, ALU.mult)
                        tt(tmp_, w1, a1, ALU.mult)
                        tt(cv, cv, tmp_, ALU.add)
                        tt(tmp_, w2, a2, ALU.mult)
                        tt(cv, cv, tmp_, ALU.add)
                        tt(cv, cv, wbb, ALU.add)
                    S.op("act", lambda e: e.activation(out=pw[:, 3:5, :], in_=pw[:, 0:2, :], func=AF.Silu),
                         reads=[b_pw], writes=[b_pw])
                    tt(pa[:, :, 0], pw[:, 3, :], fb0, ALU.mult)
                    tt(pa[:, :, 1], pw[:, 4, :], fb1, ALU.mult)
                    S.dma("sp", ACTT[:, :, 0:2].rearrange("c p t -> p c t"), pa[:], reads=[b_pw])
                S.barrier()

            if c.stop == 'P6a':
                return True
            with ExitStack() as ph:
                FC = c.FC
                GW = 512
                NP = 4
                base, rem = FC // NP, FC % NP
                pieces = []
                k0 = 0
                for p_ in range(NP):
                    nk = base + (1 if p_ < rem else 0)
                    pieces.append((k0, nk))
                    k0 += nk
                KPM = max(nk for _, nk in pieces)
                NWB = 4
                aT = sb(ph, "aT", [128, FC, 512], BF16)
                b_aT = Buf()
                wb7 = [sb(ph, "w7%d" % i, [128, KPM, GW], BF16) for i in range(NWB)]
                bwb7 = [Buf() for _ in range(NWB)]
                g2b = sb(ph, "g2b", [128, D], F32)
                b_g2b = Buf()
                S.dma("sp", g2b[:], MODS[l, 5 * D:6 * D].partition_broadcast(128), writes=[b_g2b])
                tf7 = Ring([(sb(ph, "tf7%d" % i, [128, GW], F32), Buf()) for i in range(2)])
                xp = Ring([(sb(ph, "xq%d" % i, [128, GW], F32), Buf()) for i in range(3)])
                seq = [(t, g, p_) for t in range(NT // 512) for g in range(D // GW) for p_ in range(NP)]
                dn = W["ffn_down"][l]

                def loadW(i):
                    t, g, p_ = seq[i]
                    k0, nk = pieces[p_]
                    S.dma("pool", wb7[i % NWB][:, 0:nk, :],
                          dn[k0 * 128:(k0 + nk) * 128, g * GW:(g + 1) * GW].rearrange("(kc p) n -> p kc n", p=128),
                          writes=[bwb7[i % NWB]])
                for i in range(min(NWB - 1, len(seq))):
                    loadW(i)
                for i, (t, g, p_) in enumerate(seq):
                    t0 = t * 512
                    if g == 0 and p_ == 0:
                        S.dma("sp", aT[:], ACTT[:, :, t0:t0 + 512].rearrange("k p t -> p k t"), writes=[b_aT])
                    if i + NWB - 1 < len(seq):
                        loadW(i + NWB - 1)
                    k0, nk = pieces[p_]
                    bset = [0, 1, 2, 3] if g % 2 == 0 else [4, 5, 6, 7]
                    for blk in range(4):
                        r0 = t0 + blk * 128
                        bank = bset[blk]

                        def mm(e, bank=bank, blk=blk, i=i, k0=k0, nk=nk, p_=p_):
                            for kc in range(nk):
                                ins = e.matmul(ps[bank][:, 0:GW], lhsT=aT[:, k0 + kc, blk * 128:(blk + 1) * 128],
                                               rhs=wb7[i % NWB][:, kc, :], start=(p_ == 0 and kc == 0),
                                               stop=(p_ == NP - 1 and kc == nk - 1))
                            return ins
                        S.op("pe", mm, reads=[b_aT, bwb7[i % NWB]], writes=[bps[bank]])
                        if p_ == NP - 1:
                            xt_, bxt = xp.next()
                            S.dma("sp", xt_[:], XS[r0:r0 + 128, g * GW:(g + 1) * GW], writes=[bxt])
                            tt, btt = tf7.next()
                            S.op("dve", lambda e, tt=tt, bank=bank, g=g: e.tensor_tensor(
                                out=tt[:], in0=ps[bank][:, 0:GW], in1=g2b[:, g * GW:(g + 1) * GW], op=ALU.mult),
                                reads=[bps[bank], b_g2b], writes=[btt])
                            S.op("dve", lambda e, tt=tt, xt_=xt_: e.tensor_tensor(
                                out=xt_[:], in0=xt_[:], in1=tt[:], op=ALU.add), reads=[btt, bxt], writes=[bxt])
                            S.dma("sp", XS[r0:r0 + 128, g * GW:(g + 1) * GW], xt_[:], reads=[bxt])
                S.barrier()

        stopped = (c.stop == 'A0')
        for l in range(L):
            if not stopped:
                stopped = bool(layer(l))

        with ExitStack() as ph:
            fg = sb(ph, "fg", [128, D], F32)
            nblk_final = 0 if stopped else NT // 128
            b_fg = Buf()
            S.dma("sp", fg[:], W["final_g"].partition_broadcast(128), writes=[b_fg])
            xr = Ring([(sb(ph, "xf%d" % i, [128, D], F32), Buf()) for i in range(2)])
            junk = sb(ph, "junkf", [128, D], BF16)
            b_junk = Buf()
            ssr = Ring([(sb(ph, "ssf%d" % i, [128, 4], F32), Buf()) for i in range(2)])
            for blk in range(nblk_final):
                r0 = blk * 128
                xb, bxb = xr.next()
                ss, bss = ssr.next()
                S.dma("sp", xb[:], XS[r0:r0 + 128, :], writes=[bxb])
                S.op("act", lambda e, xb=xb, ss=ss: e.activation(out=junk[:], in_=xb[:], func=AF.Square,
                                                                 accum_out=ss[:, 0:1]),
                     reads=[bxb], writes=[b_junk, bss])
                S.op("act", lambda e, ss=ss: e.activation(out=ss[:, 1:2], in_=ss[:, 0:1], func=AF.Sqrt,
                                                          scale=1.0 / D, bias=1e-6), reads=[bss], writes=[bss])
                S.op("dve", lambda e, ss=ss: e.reciprocal(out=ss[:, 2:3], in_=ss[:, 1:2]), reads=[bss], writes=[bss])
                S.op("dve", lambda e, xb=xb, ss=ss: e.scalar_tensor_tensor(
                    out=xb[:], in0=xb[:], scalar=ss[:, 2:3], in1=fg[:], op0=ALU.mult, op1=ALU.mult),
                    reads=[bxb, bss, b_fg], writes=[bxb])
                S.dma("sp", y_out[r0:r0 + 128, :], xb[:], reads=[bxb])
            S.barrier()

        S.emit(block)
        print("[build] ops=%d waits=%d  pe=%d act=%d dve=%d pool=%d sp=%d" % (
            S.n_ops, S.n_waits, len(S.ops["pe"]), len(S.ops["act"]), len(S.ops["dve"]), len(S.ops["pool"]),
            len(S.ops["sp"])), flush=True)
    return nc


def run_cfg(c, inputs, n_cores=8):
    if c.pair:
        return run_pair(c, inputs, n_cores)
    nc = build_program(c)
    consts = make_consts(c)
    x = np.asarray(inputs["x"], dtype=np.float32)
    cvec = np.asarray(inputs["c"], dtype=np.float32)
    shared = {n: np.ascontiguousarray(np.asarray(inputs[n], dtype=np.float32)) for n in W_NAMES}
    in_maps = []
    for i in range(n_cores):
        b = i % c.B
        m = {"x": np.ascontiguousarray(x[b]), "c": np.ascontiguousarray(cvec[b:b + 1])}
        m.update(shared)
        m.update(consts)
        in_maps.append(m)
    res = run_bass_kernel_spmd(nc, in_maps, core_ids=list(range(n_cores)))
    out = np.stack([np.asarray(res.results[b]["y"]) for b in range(c.B)], axis=0)
    return out.astype(np.float32)


def run_pair(c, inputs, n_cores=8):
    assert n_cores == 2 * c.B
    nc = build_program(c)
    x = np.asarray(inputs["x"], dtype=np.float32)
    cvec = np.asarray(inputs["c"], dtype=np.float32)
    shared = {n: np.ascontiguousarray(np.asarray(inputs[n], dtype=np.float32)) for n in W_NAMES}
    consts = [make_consts(c, 0), make_consts(c, 1)]
    in_maps = []
    for i in range(n_cores):
        b, half = i // 2, i % 2
        m = {"x": np.ascontiguousarray(x[b, half * c.NT:(half + 1) * c.NT]), "c": np.ascontiguousarray(cvec[b:b + 1])}
        m.update(shared)
        m.update(consts[half])
        in_maps.append(m)
    res = run_bass_kernel_spmd(nc, in_maps, core_ids=list(range(n_cores)))
    out = np.empty((c.B, c.S, c.D), np.float32)
    for i in range(n_cores):
        b, half = i // 2, i % 2
        out[b, half * c.NT:(half + 1) * c.NT] = np.asarray(res.results[i]["y"])
    return out


def kernel(**inputs):
    c = Cfg(D=4096, S=4096, B=4, DEPTH=2, pair=True)
    return run_cfg(c, inputs)
```
